# Optimizing a Trainium2 kernel written in Bass

```python
import math
import jax
import jax.numpy as jnp
from jax import lax
import numpy as np

D_MODEL = 1024
BATCH = 32
SEQ = 256
DEPTH = 2
DEC_BATCH = 8
DEC_SEQ = 4096
PAST_LEN = 512

GRID_W = 64
N_SSD_LAYERS = (DEPTH + 1) // 2
N_FNET_LAYERS = DEPTH // 2
SSD_HEAD_DIM = 64
SSD_HEADS = D_MODEL // SSD_HEAD_DIM
D_SSD = SSD_HEADS * SSD_HEAD_DIM
SSD_STATE = 128
SSD_GROUPS = 4
SSD_CONV = 5
SSD_CHUNK = 128
CONF_CH = D_MODEL
CONF_KERNEL = 31
FNET_GROUPS = 8
D_FF = ((8 * D_MODEL // 3 + 127) // 128) * 128
FFN_CONV = 3
N_MOD = 6
EPS = 1e-6
D_XBC = D_SSD + 2 * SSD_GROUPS * SSD_STATE
D_IN = D_SSD + D_XBC + 2 * SSD_HEADS + 2 * CONF_CH
DT_MIN = 1e-3
DT_MAX = 1e-1

kernel_name = "hybrid_ssd_conformer_fnet_prefix_step"


def rmsnorm(x, g):
    xf = x.astype(jnp.float32)
    y = xf * lax.rsqrt(jnp.mean(xf * xf, axis=-1, keepdims=True) + EPS)
    return (y * g.astype(jnp.float32)).astype(x.dtype)


def layernorm(x, g, b):
    xf = x.astype(jnp.float32)
    mu = jnp.mean(xf, axis=-1, keepdims=True)
    xc = xf - mu
    var = jnp.mean(xc * xc, axis=-1, keepdims=True)
    return (xc * lax.rsqrt(var + EPS) * g.astype(jnp.float32) + b.astype(jnp.float32)).astype(x.dtype)


def dwconv1d(x, w):
    k = w.shape[0]
    return lax.conv_general_dilated(
        x, w.astype(x.dtype)[:, None, :], window_strides=(1,), padding=[(k // 2, k // 2)],
        dimension_numbers=('NWC', 'WIO', 'NWC'), feature_group_count=x.shape[-1])


def dwconv2d(x, w):
    kh, kw = w.shape[0], w.shape[1]
    return lax.conv_general_dilated(
        x, w.astype(x.dtype)[:, :, None, :], window_strides=(1, 1),
        padding=[(kh // 2, kh // 2), (kw // 2, kw // 2)],
        dimension_numbers=('NHWC', 'HWIO', 'NHWC'), feature_group_count=x.shape[-1])


def ssd_scan(x, dt, a, bm, cm, h0):
    b, l, nh, p = x.shape
    g, n = bm.shape[2], bm.shape[3]
    r = nh // g
    q = SSD_CHUNK
    nc = l // q
    xd = (x * dt[..., None]).reshape(b, nc, q, g, r, p)
    la = (dt * a).reshape(b, nc, q, g, r)
    bm = bm.reshape(b, nc, q, g, n)
    cm = cm.reshape(b, nc, q, g, n)
    a_cs = jnp.cumsum(la, axis=2)
    causal = jnp.tril(jnp.ones((q, q), dtype=bool))[None, None, :, :, None, None]
    seg = a_cs[:, :, :, None] - a_cs[:, :, None, :]
    decay = jnp.exp(jnp.where(causal, seg, -jnp.inf))
    cb = jnp.einsum('bcqgn,bcsgn->bcqsg', cm, bm)
    y_diag = jnp.einsum('bcqsg,bcqsgr,bcsgrp->bcqgrp', cb, decay, xd)
    decay_to_end = jnp.exp(a_cs[:, :, -1:] - a_cs)
    chunk_states = jnp.einsum('bcqgn,bcqgr,bcqgrp->bcgrpn', bm, decay_to_end, xd)
    chunk_decay = jnp.exp(a_cs[:, :, -1])

    def step(h, inp):
        s, d = inp
        return h * d[..., None, None] + s, h

    h_last, h_prev = lax.scan(step, h0.reshape(b, g, r, p, n),
                              (jnp.moveaxis(chunk_states, 1, 0), jnp.moveaxis(chunk_decay, 1, 0)))
    y_off = jnp.einsum('bcqgn,cbgrpn,bcqgr->bcqgrp', cm, h_prev, jnp.exp(a_cs))
    y = (y_diag + y_off).reshape(b, l, nh, p)
    return y, h_last.reshape(b, nh, p, n)


def ssd_conformer_mixer(h, h0, w_in, conv_w, conv_b, dt_bias, a_log, d_skip, norm_g,
                        cconv_w, cconv_b, ln_g, ln_b, w_out):
    bsz, l, _ = h.shape
    proj = h @ w_in
    z = proj[..., :D_SSD]
    xbc = proj[..., D_SSD:D_SSD + D_XBC]
    dt_raw = proj[..., D_SSD + D_XBC:D_SSD + D_XBC + 2 * SSD_HEADS]
    glu = proj[..., D_SSD + D_XBC + 2 * SSD_HEADS:]
    xbc = jax.nn.silu(dwconv1d(xbc, conv_w) + conv_b).astype(jnp.float32)
    gn = SSD_GROUPS * SSD_STATE
    xs = xbc[..., :D_SSD].reshape(bsz, l, SSD_HEADS, SSD_HEAD_DIM)
    bm = xbc[..., D_SSD:D_SSD + gn].reshape(bsz, l, SSD_GROUPS, SSD_STATE)
    cm = xbc[..., D_SSD + gn:].reshape(bsz, l, SSD_GROUPS, SSD_STATE)
    dt = jax.nn.softplus(dt_raw.astype(jnp.float32).reshape(bsz, l, 2, SSD_HEADS)
                         + dt_bias.astype(jnp.float32))
    a = -jnp.exp(a_log.astype(jnp.float32))
    h0 = h0.astype(jnp.float32)
    y_f, s_f = ssd_scan(xs, dt[:, :, 0], a[0], bm, cm, h0[:, 0])
    flip = lambda t: jnp.flip(t, axis=1)
    y_b, s_b = ssd_scan(flip(xs), flip(dt[:, :, 1]), a[1], flip(bm), flip(cm), h0[:, 1])
    y = y_f + flip(y_b) + xs * d_skip.astype(jnp.float32)[:, None]
    y = y.reshape(bsz, l, D_SSD) * jax.nn.silu(z.astype(jnp.float32))
    y_ssd = rmsnorm(y, norm_g).astype(h.dtype)
    u = glu[..., :CONF_CH] * jax.nn.sigmoid(glu[..., CONF_CH:])
    u = dwconv1d(u, cconv_w) + cconv_b
    u = jax.nn.silu(layernorm(u, ln_g, ln_b))
    out = jnp.concatenate([y_ssd, u], axis=-1) @ w_out
    return out, jnp.stack([s_f, s_b], axis=1)


def fourier_mixer(h, w, b):
    bsz, l, d = h.shape
    u = h.astype(jnp.float32).reshape(bsz, l, FNET_GROUPS, d // FNET_GROUPS)
    u = jnp.fft.fftn(u, axes=(1, 3), norm='ortho').real
    return u.reshape(bsz, l, d).astype(h.dtype) @ w + b


def conv_ffn(h, w_up, conv_w, w_down, grid_rows):
    bsz, l, _ = h.shape
    u = h @ w_up
    if grid_rows is None:
        u = dwconv1d(u, conv_w[FFN_CONV // 2])
    else:
        u = dwconv2d(u.reshape(bsz, grid_rows, GRID_W, -1), conv_w).reshape(bsz, l, -1)
    val, gate = jnp.split(u, 2, axis=-1)
    return (val * jax.nn.silu(gate)) @ w_down


def trunk(x, cond, h0, grid_rows, p):
    states = []
    for i in range(DEPTH):
        mod = (jax.nn.silu(cond) @ p['w_mod'][i] + p['b_mod'][i]).astype(x.dtype)
        sh1, sc1, gt1, sh2, sc2, gt2 = jnp.split(mod[:, None, :], N_MOD, axis=-1)
        hm = rmsnorm(x, p['g_mix'][i]) * (1 + sc1) + sh1
        j = i // 2
        if i % 2 == 0:
            out, st = ssd_conformer_mixer(
                hm, h0[:, j], p['w_in'][j], p['ssd_conv_w'][j], p['ssd_conv_b'][j],
                p['ssd_dt_bias'][j], p['ssd_a_log'][j], p['ssd_d'][j], p['ssd_norm'][j],
                p['conf_conv_w'][j], p['conf_conv_b'][j], p['conf_ln_g'][j], p['conf_ln_b'][j],
                p['w_out'][j])
            if grid_rows is None:
                states.append(st)
        else:
            out = fourier_mixer(hm, p['fnet_w'][j], p['fnet_b'][j])
        x = x + gt1 * out
        hf = rmsnorm(x, p['g_ffn'][i]) * (1 + sc2) + sh2
        x = x + gt2 * conv_ffn(hf, p['ffn_w_up'][i], p['ffn_conv_w'][i], p['ffn_w_down'][i], grid_rows)
    return rmsnorm(x, p['norm_f']), states


def setup_inputs(seed: int = 0) -> dict:
    key = jax.random.key(seed)
    ks = jax.random.split(key, 32)
    nrm = lambda k, shape, s: jax.random.normal(k, shape, jnp.float32) * s
    dt0 = jnp.exp(jax.random.uniform(ks[12], (N_SSD_LAYERS, 2, SSD_HEADS), jnp.float32)
                  * (math.log(DT_MAX) - math.log(DT_MIN)) + math.log(DT_MIN))
    return {
        'x_prompt': nrm(ks[0], (BATCH, SEQ, D_MODEL), 1.0),
        'x_sample': nrm(ks[1], (DEC_BATCH, DEC_SEQ, D_MODEL), 1.0),
        'c': nrm(ks[2], (DEC_BATCH, D_MODEL), 1.0),
        'state_ssd': nrm(ks[3], (DEC_BATCH, N_SSD_LAYERS, 2, SSD_HEADS, SSD_HEAD_DIM, SSD_STATE), 0.1),
        'c_ctx': nrm(ks[4], (D_MODEL,), 1.0),
        'w_mod': nrm(ks[5], (DEPTH, D_MODEL, N_MOD * D_MODEL), 0.5 * D_MODEL ** -0.5),
        'b_mod': nrm(ks[6], (DEPTH, N_MOD * D_MODEL), 0.01),
        'g_mix': 1.0 + nrm(ks[7], (DEPTH, D_MODEL), 0.02),
        'g_ffn': 1.0 + nrm(ks[8], (DEPTH, D_MODEL), 0.02),
        'w_in': nrm(ks[9], (N_SSD_LAYERS, D_MODEL, D_IN), D_MODEL ** -0.5),
        'ssd_conv_w': nrm(ks[10], (N_SSD_LAYERS, SSD_CONV, D_XBC), SSD_CONV ** -0.5),
        'ssd_conv_b': nrm(ks[11], (N_SSD_LAYERS, D_XBC), 0.01),
        'ssd_dt_bias': dt0 + jnp.log(-jnp.expm1(-dt0)),
        'ssd_a_log': jnp.log(jax.random.uniform(ks[13], (N_SSD_LAYERS, 2, SSD_HEADS), jnp.float32, 1.0, 16.0)),
        'ssd_d': 1.0 + nrm(ks[14], (N_SSD_LAYERS, SSD_HEADS), 0.1),
        'ssd_norm': 1.0 + nrm(ks[15], (N_SSD_LAYERS, D_SSD), 0.02),
        'conf_conv_w': nrm(ks[16], (N_SSD_LAYERS, CONF_KERNEL, CONF_CH), CONF_KERNEL ** -0.5),
        'conf_conv_b': nrm(ks[17], (N_SSD_LAYERS, CONF_CH), 0.01),
        'conf_ln_g': 1.0 + nrm(ks[18], (N_SSD_LAYERS, CONF_CH), 0.02),
        'conf_ln_b': nrm(ks[19], (N_SSD_LAYERS, CONF_CH), 0.01),
        'w_out': nrm(ks[20], (N_SSD_LAYERS, D_SSD + CONF_CH, D_MODEL), (D_SSD + CONF_CH) ** -0.5),
        'fnet_w': nrm(ks[21], (N_FNET_LAYERS, D_MODEL, D_MODEL), D_MODEL ** -0.5),
        'fnet_b': nrm(ks[22], (N_FNET_LAYERS, D_MODEL), 0.01),
        'ffn_w_up': nrm(ks[23], (DEPTH, D_MODEL, 2 * D_FF), D_MODEL ** -0.5),
        'ffn_conv_w': nrm(ks[24], (DEPTH, FFN_CONV, FFN_CONV, 2 * D_FF), 1.0 / FFN_CONV),
        'ffn_w_down': nrm(ks[25], (DEPTH, D_FF, D_MODEL), D_FF ** -0.5),
        'norm_f': 1.0 + nrm(ks[26], (D_MODEL,), 0.02),
    }


def reference(x_prompt, x_sample, c, state_ssd, c_ctx, w_mod, b_mod, g_mix, g_ffn, w_in,
              ssd_conv_w, ssd_conv_b, ssd_dt_bias, ssd_a_log, ssd_d, ssd_norm,
              conf_conv_w, conf_conv_b, conf_ln_g, conf_ln_b, w_out, fnet_w, fnet_b,
              ffn_w_up, ffn_conv_w, ffn_w_down, norm_f):
    p = dict(w_mod=w_mod, b_mod=b_mod, g_mix=g_mix, g_ffn=g_ffn, w_in=w_in,
             ssd_conv_w=ssd_conv_w, ssd_conv_b=ssd_conv_b, ssd_dt_bias=ssd_dt_bias,
             ssd_a_log=ssd_a_log, ssd_d=ssd_d, ssd_norm=ssd_norm,
             conf_conv_w=conf_conv_w, conf_conv_b=conf_conv_b, conf_ln_g=conf_ln_g,
             conf_ln_b=conf_ln_b, w_out=w_out, fnet_w=fnet_w, fnet_b=fnet_b,
             ffn_w_up=ffn_w_up, ffn_conv_w=ffn_conv_w, ffn_w_down=ffn_w_down, norm_f=norm_f)
    h0_ctx = jnp.zeros((x_prompt.shape[0], N_SSD_LAYERS, 2, SSD_HEADS, SSD_HEAD_DIM, SSD_STATE),
                       jnp.float32)
    y_prompt, ctx_states = trunk(x_prompt, c_ctx[None, :], h0_ctx, None, p)
    new_state_ssd = jnp.stack(ctx_states, axis=1).astype(x_prompt.dtype)
    grid_rows = x_sample.shape[1] // GRID_W
    y_sample, _ = trunk(x_sample, c, state_ssd, grid_rows, p)
    return (y_prompt, y_sample, new_state_ssd)
```

```python
import contextlib
import numpy as np
import ml_dtypes
import concourse.bass as bass
import concourse.mybir as mybir
from concourse.bass_utils import run_bass_kernel_spmd

F32 = mybir.dt.float32
BF16 = mybir.dt.bfloat16
AF = mybir.ActivationFunctionType
ALU = mybir.AluOpType

D = 1024
DIN = 5152
DFF = 2816
EPS = 1e-6
PAD = 128
ORDER9 = [(0, 1), (1, 1), (2, 1), (0, 0), (0, 2), (1, 0), (1, 2), (2, 0), (2, 2)]
import os
LATE = os.environ.get("LATE", "1") == "1"
SAME_ENGINE_SYNC = set(os.environ.get("SAME_SYNC", "dve").split(","))


class Buf:
    __slots__ = ("w", "r")

    def __init__(self):
        self.w = None
        self.r = {}


class T:
    def __init__(self, t):
        self.t = t
        self.b = Buf()

    def __getitem__(self, k):
        return self.t[k]


class Sched:
    def __init__(self, nc):
        self.nc = nc
        self.E = {"pe": nc.tensor, "act": nc.scalar, "dve": nc.vector, "pool": nc.gpsimd, "sp": nc.sync}
        self.sem = {k: nc.alloc_semaphore(name="es_" + k) for k in self.E}
        self.cnt = {k: 0 for k in self.E}
        self.seen = {k: {} for k in self.E}
        self.rings = {}
        for q, n in (("sp", 24), ("act", 8), ("pool", 4)):
            self.rings[q] = dict(sems=[nc.alloc_semaphore(name="ds_%s%d" % (q, i)) for i in range(n)],
                                 uses=[0] * n, n=0)

    def semof(self, k):
        if isinstance(k, str):
            return self.sem[k]
        return self.rings[k[1]]["sems"][k[2]]

    def _waits(self, e, r, w, extra=()):
        need = {}

        def add(ev):
            if ev is None:
                return
            k, v = ev
            if k == e and e not in SAME_ENGINE_SYNC:
                return
            if self.seen[e].get(k, 0) >= v:
                return
            if need.get(k, 0) < v:
                need[k] = v

        for b in r:
            add(b.w)
        for b in w:
            add(b.w)
            for k, v in b.r.items():
                add((k, v))
        for ev in extra:
            add(ev)
        eng = self.E[e]
        for k, v in need.items():
            eng.wait_ge(self.semof(k), v)
            self.seen[e][k] = v

    @staticmethod
    def _bufs(xs):
        return [x.b if isinstance(x, T) else x for x in xs]

    def _commit(self, ev, r, w):
        for b in w:
            b.w = ev
            b.r = {}
        for b in r:
            if b.r.get(ev[0], 0) < ev[1]:
                b.r[ev[0]] = ev[1]

    def op(self, e, fn, r=(), w=(), inc=True):
        r = self._bufs(r)
        w = self._bufs(w)
        self._waits(e, r, w)
        ins = fn(self.E[e])
        if inc:
            self.cnt[e] += 1
            ins.then_inc(self.sem[e], 1)
            ev = (e, self.cnt[e])
        else:
            ev = (e, self.cnt[e] + 1)
        self._commit(ev, r, w)

    def dma(self, q, out, in_, r=(), w=()):
        r = self._bufs(r)
        w = self._bufs(w)
        ring = self.rings[q]
        slot = ring["n"] % len(ring["sems"])
        ring["n"] += 1
        key = ("d", q, slot)
        extra = []
        if ring["uses"][slot] > 0:
            extra.append((key, 16 * ring["uses"][slot]))
        self._waits(q, r, w, extra)
        ins = self.E[q].dma_start(out=out, in_=in_)
        ring["uses"][slot] += 1
        ins.then_inc(ring["sems"][slot], 16)
        ev = (key, 16 * ring["uses"][slot])
        self._commit(ev, r, w)

    def all_events(self):
        evs = [(k, v) for k, v in self.cnt.items() if v > 0]
        for q, ring in self.rings.items():
            for i, u in enumerate(ring["uses"]):
                if u > 0:
                    evs.append((("d", q, i), 16 * u))
        return evs

    def barrier(self, engines=None):
        evs = self.all_events()
        for e in (engines or self.E):
            eng = self.E[e]
            for k, v in evs:
                if self.seen[e].get(k, 0) >= v:
                    continue
                eng.wait_ge(self.semof(k), v)
                self.seen[e][k] = v


def build(seqs, upto=None, debug=False):
    nc = bass.Bass("TRN2", target_bir_lowering=False)
    S = Sched(nc)
    nseq = len(seqs)
    NP = sum(1 for k, _ in seqs if k == "p")
    tok0, ptok0 = [], []
    t = 0
    pt = PAD
    for k, L in seqs:
        tok0.append(t)
        ptok0.append(pt)
        t += L
        pt += L + PAD
    NTOK, NPTOK = t, pt
    NCH = NTOK // 128
    Ls = sorted(set(L for _, L in seqs))

    def din(name, shape, dt=F32):
        return nc.dram_tensor(name, list(shape), dt, kind="ExternalInput").ap()

    def dscr(name, shape, dt=F32):
        return nc.dram_tensor(name, list(shape), dt, kind="ExternalOutput" if debug else "Internal").ap()

    def finish():
        S.barrier(["sp"])
        ES.close()
        return nc

    x_in = din("x", [NTOK, D])
    condT = din("condT", [128, 8, 2])
    state0 = din("state0", [2, D, 128])
    w_mod = din("w_mod", [2, D, 6 * D])
    b_mod = din("b_mod", [1, 12 * D])
    g_rows = din("g_rows", [1, 4 * D])
    w_in = din("w_in", [D, DIN])
    cwT = din("ssd_conv_wT", [128, 16, 5])
    cbT = din("ssd_conv_bT", [128, 16])
    dtb = din("dt_bias", [1, 32])
    alog = din("a_log", [1, 32])
    drow = din("d_row", [1, D])
    ngrow = din("ssd_norm", [1, D])
    ccwT = din("conf_conv_wT", [128, 8, 31])
    ccbT = din("conf_conv_bT", [128, 8])
    lngT = din("ln_gT", [128, 8])
    lnbT = din("ln_bT", [128, 8])
    w_out = din("w_out", [2 * D, D])
    fnet_w = din("fnet_w", [D, D])
    fnet_b = din("fnet_b", [1, D])
    w_up = din("ffn_w_up", [2, D, 2 * DFF])
    fcwT = din("ffn_conv_wT", [2, 128, 22, 18])
    w_down = din("ffn_w_down", [2, DFF, D])
    nf_row = din("norm_f", [1, D])
    cbf = din("consts_bf", [128, 9, 128], BF16)
    cf32 = din("consts_f32", [128, 128])
    ccs = din("dft_c", [128, 2, 128], BF16)
    dftL = {L: (din("dft_cl_%d" % L, [L // 256, 128, L // 128, 256], BF16),
                din("dft_sl_%d" % L, [L // 256, 128, L // 128, 256], BF16)) for L in Ls}

    y_out = nc.dram_tensor("y", [NTOK, D], F32, kind="ExternalOutput").ap()
    st_out = nc.dram_tensor("st", [max(NP, 1), 2, D, 128], F32, kind="ExternalOutput").ap()

    MODROW = dscr("modrow", [2, 12 * D])
    ZS = dscr("zs", [NTOK, D])
    YP = dscr("yp", [NTOK, D])
    XA = dscr("xa", [NTOK, D])
    XB = dscr("xb", [NTOK, D])
    XBCR = dscr("xbcr", [2 * D, NPTOK], BF16)
    GLU = dscr("glu", [D, NPTOK], BF16)
    UR = dscr("ur", [2 * DFF, NPTOK], BF16)
    CTS = dscr("cts", [NCH, 128, 2560], BF16)

    ES = contextlib.ExitStack()

    def sb(shape, dt=F32, stack=None):
        return T((stack or ES).enter_context(nc.sbuf_tensor(nc.make_name("sb", True), list(shape), dt)))

    def ps(shape, dt=F32):
        return T(ES.enter_context(nc.psum_tensor(nc.make_name("ps", True), list(shape), dt)))

    B = [ps([128, 512]) for _ in range(7)]
    TB = ps([128, 1024], BF16)

    CB_ = sb([128, 9, 128], BF16)
    CF = sb([128, 128])
    CCS = sb([128, 2, 128], BF16)
    S.dma("sp", CB_[:], cbf, w=[CB_])
    S.dma("sp", CF[:], cf32, w=[CF])
    S.dma("sp", CCS[:], ccs, w=[CCS])
    IDB = CB_[:, 0, :]
    TRI = {0: (CB_[:, 1, :], CB_[:, 2, :]), 1: (CB_[:, 3, :], CB_[:, 4, :])}
    NEGM = {0: CB_[:, 5, :], 1: CB_[:, 6, :]}
    ONESB = CB_[:, 7, :]
    ONESM = CB_[:, 8, :]
    IDF = CF[:, :]

    DT = sb([128, NCH, 32])
    LA = sb([128, NCH, 32])
    LAH = sb([128, NCH, 32], BF16)
    LAL = sb([128, NCH, 32], BF16)
    ABC = sb([128, 32])
    EPSC = sb([128, 2])
    S.op("pool", lambda e: e.memset(EPSC[:, 0:1], EPS), w=[EPSC])
    S.op("pool", lambda e: e.memset(EPSC[:, 1:2], 1.0), w=[EPSC])
    DTBB = sb([128, 32])
    WCH = 1408
    WST = [sb([128, WCH]) for _ in range(2)]
    wst_n = [0]

    def bc(ap_row, n=128):
        return ap_row.partition_broadcast(n)

    S.dma("sp", ABC[:], bc(alog), w=[ABC])
    S.dma("sp", DTBB[:], bc(dtb), w=[DTBB])
    S.op("act", lambda e: e.activation(out=ABC[:], in_=ABC[:], func=AF.Exp), r=[ABC], w=[ABC])
    S.op("dve", lambda e: e.tensor_scalar(out=ABC[:], in0=ABC[:], scalar1=-1.0, scalar2=None, op0=ALU.mult),
         r=[ABC], w=[ABC])

    cvt_rr = [0]

    class WLoad:
        def __init__(self, dst, src2d, K, N, bounds=None):
            self.bounds = bounds or [0, N]
            self.bufs = [Buf() for _ in range(len(self.bounds) - 1)]
            self.jobs = []
            self.pos = 0
            for b_ in range(len(self.bounds) - 1):
                for k in range(K):
                    c0 = self.bounds[b_]
                    while c0 < self.bounds[b_ + 1]:
                        c1 = min(self.bounds[b_ + 1], c0 + WCH)
                        self.jobs.append((b_, self._mk(dst, src2d, k, c0, c1, self.bufs[b_])))
                        c0 = c1

        def _mk(self, dst, src2d, k, c0, c1, buf):
            def job():
                st = WST[wst_n[0] % 2]
                wst_n[0] += 1
                S.dma("sp", st[:, 0:c1 - c0], src2d[k * 128:(k + 1) * 128, c0:c1], w=[st])
                eng = ("act", "dve")[cvt_rr[0] % 2]
                cvt_rr[0] += 1
                if eng == "act":
                    S.op("act", lambda e: e.activation(out=dst[:, k, c0:c1], in_=st[:, 0:c1 - c0], func=AF.Copy),
                         r=[st], w=[buf])
                else:
                    S.op("dve", lambda e: e.tensor_copy(out=dst[:, k, c0:c1], in_=st[:, 0:c1 - c0]), r=[st], w=[buf])
            return job

        def pump(self, n=1):
            for _ in range(n):
                if self.pos < len(self.jobs):
                    self.jobs[self.pos][1]()
                    self.pos += 1

        def need(self, b_):
            while self.pos < len(self.jobs) and self.jobs[self.pos][0] <= b_:
                self.pump(1)

        def all(self):
            self.pump(len(self.jobs))

        def rb(self, col):
            for i in range(len(self.bufs)):
                if self.bounds[i] <= col < self.bounds[i + 1]:
                    return self.bufs[i]
            raise ValueError(col)

    with contextlib.ExitStack() as ph:
        Z = sb([128, 16, PAD], F32, ph)
        Zb = sb([128, 44, PAD], BF16, ph)
        S.op("pool", lambda e: e.memset(Z[:], 0.0), w=[Z])
        S.op("pool", lambda e: e.memset(Zb[:], 0.0), w=[Zb])
        for scr, C, zt_ in ((XBCR, 16, Zb), (GLU, 8, Zb), (UR, 44, Zb)):
            v = scr.rearrange("(j p) t -> p j t", p=128)
            for s in range(nseq + 1):
                p0 = (ptok0[s] - PAD) if s < nseq else (NPTOK - PAD)
                S.dma("sp", v[:, :, p0:p0 + PAD], zt_[:, 0:C, :], r=[zt_])

        scT = sb([128, 8, 2], F32, ph)
        S.dma("sp", scT[:], condT, w=[scT])
        S.op("act", lambda e: e.activation(out=scT[:], in_=scT[:], func=AF.Silu), r=[scT], w=[scT])
        grow = sb([2, 4 * D], F32, ph)
        S.dma("sp", grow[:], bc(g_rows, 2), w=[grow])
        brow = sb([2, 6 * D], F32, ph)
        modrow = sb([2, 6 * D], F32, ph)
        wm = [sb([128, 8, 512], F32, ph) for _ in range(5)]
        wn_ = 0
        for i in range(2):
            S.dma("sp", brow[:], bc(b_mod[:, i * 6 * D:(i + 1) * 6 * D], 2), w=[brow])
            for m in range(6):
                for half in range(2):
                    slot = wm[wn_ % 5]
                    wn_ += 1
                    o = m * D + half * 512
                    S.dma("sp", slot[:], w_mod[i][:, o:o + 512].rearrange("(k p) n -> p k n", p=128), w=[slot])
                    bk = B[half]
                    for k in range(8):
                        S.op("pe", lambda e, k=k, slot=slot, bk=bk: e.matmul(
                            bk[0:2, :], scT[:, k, :], slot[:, k, :],
                            start=(k == 0), stop=(k == 7)), r=[scT, slot], w=[bk], inc=(k == 7))
                    S.op("dve", lambda e, o=o, bk=bk: e.tensor_tensor(
                        out=modrow[:, o:o + 512], in0=bk[0:2, :], in1=brow[:, o:o + 512], op=ALU.add),
                        r=[bk, brow], w=[modrow])
            for (m, gi) in ((1, i), (4, 2 + i)):
                o = m * D
                S.op("dve", lambda e, o=o, gi=gi: e.scalar_tensor_tensor(
                    out=modrow[:, o:o + D], in0=modrow[:, o:o + D], scalar=1.0, in1=grow[:, gi * D:(gi + 1) * D],
                    op0=ALU.add, op1=ALU.mult), r=[modrow, grow], w=[modrow])
            S.dma("sp", MODROW[:, i * 6 * D:(i + 1) * 6 * D], modrow[:], r=[modrow])
        S.barrier()
    if upto == "k1":
        return finish()

    def modrow_ap(cond, layer, m):
        o = (layer * 6 + m) * D
        return MODROW[cond:cond + 1, o:o + D]

    cond_of = [0 if k == "s" else 1 for k, _ in seqs]

    class CondTiles:
        def __init__(self, ph, specs):
            self.t = [sb([128, D], F32, ph) for _ in specs]
            self.specs = specs
            self.cur = None

        def set(self, c):
            if c == self.cur:
                return
            self.cur = c
            for t_, (layer, m) in zip(self.t, self.specs):
                S.dma("sp", t_[:], bc(modrow_ap(c, layer, m)), w=[t_])

    PTMP = [sb([128, 2]) for _ in range(2)]
    ptn = [0]

    def fma(eng, accT, oview, i_ap, wcol, rT, first, bias=None):
        o_ap = oview(accT)
        if first:
            if bias is not None:
                S.op(eng, lambda e: e.tensor_scalar(out=o_ap, in0=i_ap, scalar1=wcol, scalar2=bias,
                                                    op0=ALU.mult, op1=ALU.add), r=rT, w=[accT])
            else:
                S.op(eng, lambda e: e.tensor_scalar(out=o_ap, in0=i_ap, scalar1=wcol, scalar2=None, op0=ALU.mult),
                     r=rT, w=[accT])
        elif eng == "dve":
            S.op(eng, lambda e: e.scalar_tensor_tensor(out=o_ap, in0=i_ap, scalar=wcol, in1=o_ap,
                                                       op0=ALU.mult, op1=ALU.add), r=rT + [accT], w=[accT])
        else:
            tmp = PTMP[ptn[0] % 2]
            ptn[0] += 1
            t_ap = oview(tmp)
            S.op(eng, lambda e: e.tensor_scalar(out=t_ap, in0=i_ap, scalar1=wcol, scalar2=None, op0=ALU.mult),
                 r=rT, w=[tmp])
            S.op(eng, lambda e: e.tensor_tensor(out=o_ap, in0=o_ap, in1=t_ap, op=ALU.add), r=[accT, tmp], w=[accT])

    def hT_jobs(src, s, t0, TT, Gc, SHc, xts, nm_tiles, hmb, hT, xcount):
        As, Bs = [], []
        nblk = TT // 128
        for blk in range(nblk):
            hb_ = hmb[blk % len(hmb)] if isinstance(hmb, list) else hmb

            def jobA(blk=blk, hb_=hb_):
                xt = xts[xcount[0] % len(xts)]
                xcount[0] += 1
                g0 = tok0[s] + t0 + blk * 128
                S.dma("sp", xt[:], src[g0:g0 + 128, :], w=[xt])
                S.op("act", lambda e: e.activation(out=nm_tiles[0][:], in_=xt[:], func=AF.Square,
                                                   accum_out=nm_tiles[1][:, 0:1]), r=[xt], w=[nm_tiles[0], nm_tiles[1]])
                ss = nm_tiles[1]
                S.op("act", lambda e: e.activation(out=ss[:, 1:2], in_=ss[:, 0:1], func=AF.Sqrt, bias=EPSC[:, 0:1], scale=1.0 / D), r=[ss, EPSC], w=[ss])
                S.op("dve", lambda e: e.reciprocal(out=ss[:, 2:3], in_=ss[:, 1:2]), r=[ss], w=[ss])
                h1 = nm_tiles[2]
                S.op("dve", lambda e: e.scalar_tensor_tensor(out=h1[:], in0=xt[:], scalar=ss[:, 2:3], in1=Gc[:],
                                                             op0=ALU.mult, op1=ALU.mult), r=[xt, ss, Gc], w=[h1])
                S.op("dve", lambda e: e.tensor_tensor(out=hb_[:], in0=h1[:], in1=SHc[:], op=ALU.add),
                     r=[h1, SHc], w=[hb_])

            def jobB(blk=blk, hb_=hb_):
                if hT is None:
                    return
                for k in range(8):
                    S.op("pe", lambda e, k=k: e.transpose(out=TB[:, k * 128:(k + 1) * 128],
                                                          in_=hb_[:, k * 128:(k + 1) * 128], identity=IDB),
                         r=[hb_, CB_], w=[TB], inc=(k == 7))
                S.op("act", lambda e: e.activation(
                    out=hT[:, :, blk * 128:(blk + 1) * 128], in_=TB[:, :].rearrange("p (k t) -> p k t", k=8),
                    func=AF.Copy), r=[TB], w=[hT])
            As.append(jobA)
            Bs.append(jobB)
        if isinstance(hmb, list) and len(hmb) > 1:
            out = [As[0]]
            for i in range(1, nblk):
                out.append(As[i])
                out.append(Bs[i - 1])
            out.append(Bs[nblk - 1])
            return out
        out = []
        for i in range(nblk):
            out += [As[i], Bs[i]]
        return out

    def make_hT(ph, src, s, t0, TT, Gc, SHc, xts, nm_tiles, hmb, hT, xcount):
        for j_ in hT_jobs(src, s, t0, TT, Gc, SHc, xts, nm_tiles, hmb, hT, xcount):
            j_()

    with contextlib.ExitStack() as ph:
        WIN = sb([128, 8, DIN], BF16, ph)
        WLw = WLoad(WIN, w_in, 8, DIN, [0, 1024, 3104, 4128, 5152])
        WLw.need(0)
        CT2 = CondTiles(ph, [(0, 1), (0, 0)])
        xts = [sb([128, D], F32, ph) for _ in range(2)]
        nm = (sb([128, D], BF16, ph), sb([128, 4], F32, ph), sb([128, D], F32, ph))
        hmb = [sb([128, D], BF16, ph) for _ in range(2)]
        hTs = [sb([128, 8, 512], BF16, ph) for _ in range(2)]
        zst = [sb([128, D], F32, ph) for _ in range(2)]
        xst = [sb([128, 8, 512], BF16, ph) for _ in range(2)]
        ust = [sb([128, 4, 512], BF16, ph) for _ in range(2)]
        sg = [sb([128, 512], F32, ph) for _ in range(2)]
        dtt = [sb([128, 32], F32, ph) for _ in range(4)]
        xc = [0]
        tn = 0
        tiles2 = [(s, kind, L, min(512, L), t0) for s, (kind, L) in enumerate(seqs) for t0 in range(0, L, min(512, L))]
        pend = []

        def pump(n=1):
            for _ in range(n):
                if pend:
                    pend.pop(0)()

        def queue_tile(ti):
            s, kind, L, TT, t0 = tiles2[ti]
            pend.append(lambda: CT2.set(cond_of[s]))
            pend.extend(hT_jobs(x_in, s, t0, TT, CT2.t[0], CT2.t[1], xts, nm, hmb, hTs[ti % 2], xc))

        queue_tile(0)
        for ti, (s, kind, L, TT, t0) in enumerate(tiles2):
            nb = TT // 128
            if True:
                pump(len(pend))
                hT = hTs[ti % 2]
                if ti + 1 < len(tiles2):
                    queue_tile(ti + 1)
                for blk in range(nb):
                    zt = zst[blk % 2]
                    for half in range(2):
                        bk = B[half]
                        for k in range(8):
                            S.op("pe", lambda e, k=k, blk=blk, half=half, bk=bk: e.matmul(
                                bk[:, :], hT[:, k, blk * 128:(blk + 1) * 128], WIN[:, k, half * 512:(half + 1) * 512],
                                start=(k == 0), stop=(k == 7)), r=[hT, WLw.rb(0)], w=[bk], inc=(k == 7))
                        S.op("act", lambda e, half=half, bk=bk, zt=zt: e.activation(
                            out=zt[:, half * 512:(half + 1) * 512], in_=bk[:, :], func=AF.Silu), r=[bk], w=[zt])
                        WLw.pump(3)
                    g0 = tok0[s] + t0 + blk * 128
                    S.dma("sp", ZS[g0:g0 + 128, :], zt[:], r=[zt])
                WLw.need(1)
                for blk in range(nb):
                    ch = (tok0[s] + t0) // 128 + blk
                    bk = B[6]
                    for k in range(8):
                        S.op("pe", lambda e, k=k, blk=blk: e.matmul(
                            bk[:, 0:32], hT[:, k, blk * 128:(blk + 1) * 128], WIN[:, k, 3072:3104],
                            start=(k == 0), stop=(k == 7)), r=[hT, WLw.rb(3072)], w=[bk], inc=(k == 7))
                    xx, mm_, aa, ee = dtt
                    S.op("dve", lambda e: e.tensor_tensor(out=xx[:], in0=bk[:, 0:32], in1=DTBB[:], op=ALU.add),
                         r=[bk, DTBB], w=[xx])
                    S.op("dve", lambda e: e.tensor_scalar(out=mm_[:], in0=xx[:], scalar1=0.0, scalar2=None, op0=ALU.max),
                         r=[xx], w=[mm_])
                    S.op("act", lambda e: e.activation(out=aa[:], in_=xx[:], func=AF.Abs), r=[xx], w=[aa])
                    S.op("act", lambda e: e.activation(out=ee[:], in_=aa[:], func=AF.Exp, scale=-1.0), r=[aa], w=[ee])
                    S.op("act", lambda e: e.activation(out=ee[:], in_=ee[:], func=AF.Ln, bias=EPSC[:, 1:2], scale=1.0), r=[ee, EPSC], w=[ee])
                    S.op("dve", lambda e, ch=ch: e.tensor_tensor(out=DT[:, ch, :], in0=mm_[:], in1=ee[:], op=ALU.add),
                         r=[mm_, ee], w=[DT])
                    S.op("dve", lambda e, ch=ch: e.tensor_tensor(out=LA[:, ch, :], in0=DT[:, ch, :], in1=ABC[:],
                                                                 op=ALU.mult), r=[DT, ABC], w=[LA])
                    S.op("dve", lambda e, ch=ch: e.tensor_copy(out=LAH[:, ch, :], in_=LA[:, ch, :]), r=[LA], w=[LAH])
                    S.op("dve", lambda e, ch=ch: e.tensor_tensor(out=LAL[:, ch, :], in0=LA[:, ch, :], in1=LAH[:, ch, :],
                                                                 op=ALU.subtract), r=[LA, LAH], w=[LAL])
                p0 = ptok0[s] + t0
                for j in range(16):
                    xs_ = xst[j // 8]
                    bk = B[j % 2]
                    c0 = D + j * 128
                    for k in range(8):
                        S.op("pe", lambda e, k=k, c0=c0, bk=bk: e.matmul(
                            bk[:, 0:TT], WIN[:, k, c0:c0 + 128], hT[:, k, 0:TT],
                            start=(k == 0), stop=(k == 7)), r=[hT, WLw.rb(c0)], w=[bk], inc=(k == 7))
                    WLw.pump(1)
                    if j % 2 == 0:
                        S.op("act", lambda e, j=j, bk=bk, xs_=xs_: e.activation(out=xs_[:, j % 8, 0:TT], in_=bk[:, 0:TT], func=AF.Copy),
                             r=[bk], w=[xs_])
                    else:
                        S.op("dve", lambda e, j=j, bk=bk, xs_=xs_: e.tensor_copy(out=xs_[:, j % 8, 0:TT], in_=bk[:, 0:TT]),
                             r=[bk], w=[xs_])
                    if j % 4 == 1:
                        pump(2 if j == 1 else 1)
                    if j % 8 == 7:
                        jb = j - 7
                        S.dma("sp", XBCR.rearrange("(j p) t -> p j t", p=128)[:, jb:jb + 8, p0:p0 + TT], xs_[:, :, 0:TT], r=[xs_])
                for j in range(8):
                    us_ = ust[j // 4]
                    ba, bb = B[2 + 2 * (j % 2)], B[3 + 2 * (j % 2)]
                    ca = 3104 + j * 128
                    cb_ = 3104 + D + j * 128
                    WLw.need(3)
                    pump(1)
                    for (bk, c0) in ((ba, ca), (bb, cb_)):
                        for k in range(8):
                            S.op("pe", lambda e, k=k, c0=c0, bk=bk: e.matmul(
                                bk[:, 0:TT], WIN[:, k, c0:c0 + 128], hT[:, k, 0:TT],
                                start=(k == 0), stop=(k == 7)), r=[hT, WLw.rb(c0)], w=[bk], inc=(k == 7))
                    sgt = sg[j % 2]
                    S.op("act", lambda e, bb=bb, sgt=sgt: e.activation(out=sgt[:, 0:TT], in_=bb[:, 0:TT], func=AF.Sigmoid),
                         r=[bb], w=[sgt])
                    S.op("dve", lambda e, j=j, ba=ba, sgt=sgt, us_=us_: e.tensor_tensor(
                        out=us_[:, j % 4, 0:TT], in0=ba[:, 0:TT], in1=sgt[:, 0:TT], op=ALU.mult), r=[ba, sgt], w=[us_])
                    if j % 4 == 3:
                        jb = j - 3
                        S.dma("sp", GLU.rearrange("(j p) t -> p j t", p=128)[:, jb:jb + 4, p0:p0 + TT], us_[:, :, 0:TT], r=[us_])
        S.barrier()

    if upto == "k2":
        return finish()

    def ssd_chunk(d, ch, xs_tok, bm_tok, bmT, cmT, rT, H, Hb, W, yfin, hook=None, xsD=None, xsDT=None, ypacc=None):
        E3, nacs, wd, xd, xdd, Ebuf, Mbuf, t1s = W["E3"], W["nacs"], W["wd"], W["xd"], W["xdd"], W["E"], W["M"], W["t1"]
        c0 = d * 16
        tiny = B[5]
        TO = 256
        for (col, lhs) in ((0, TRI[d][0]), (16, TRI[d][1]), (32, ONESB)):
            S.op("pe", lambda e, col=col, lhs=lhs: e.matmul(tiny[:, TO + col:TO + col + 16], lhs, LAH[:, ch, c0:c0 + 16],
                                                            start=True, stop=False), r=[CB_, LAH], w=[tiny], inc=False)
            S.op("pe", lambda e, col=col, lhs=lhs: e.matmul(tiny[:, TO + col:TO + col + 16], lhs, LAL[:, ch, c0:c0 + 16],
                                                            start=False, stop=True), r=[CB_, LAL], w=[tiny], inc=True)
        S.op("act", lambda e: e.activation(out=E3[:], in_=tiny[:, TO:TO + 48], func=AF.Exp), r=[tiny], w=[E3])
        S.op("dve", lambda e: e.tensor_scalar(out=nacs[:], in0=tiny[:, TO:TO + 16], scalar1=-1.0, scalar2=None, op0=ALU.mult),
             r=[tiny], w=[nacs])
        S.op("dve", lambda e: e.tensor_tensor(out=wd[:], in0=DT[:, ch, c0:c0 + 16], in1=E3[:, 16:32], op=ALU.mult),
             r=[DT, E3], w=[wd])
        xs3 = xs_tok.rearrange("p (h q) -> p h q", h=16)
        S.op("dve", lambda e: e.tensor_tensor(out=xd[:, :].rearrange("p (h q) -> p h q", h=16), in0=xs3,
                                              in1=DT[:, ch, c0:c0 + 16].unsqueeze(2).to_broadcast([128, 16, 64]),
                                              op=ALU.mult), r=rT + [DT], w=[xd])
        def late_dve(which):
            if which == 0:
                S.op("dve", lambda e: e.tensor_tensor(out=xdd[:, :].rearrange("p (h q) -> p h q", h=16), in0=xs3,
                                                      in1=wd[:, :].unsqueeze(2).to_broadcast([128, 16, 64]),
                                                      op=ALU.mult), r=rT + [wd], w=[xdd])
            else:
                S.op("dve", lambda e: e.tensor_tensor(
                    out=H[:, :].rearrange("p (h q) -> p h q", h=16), in0=H[:, :].rearrange("p (h q) -> p h q", h=16),
                    in1=E3[:, 32:48].unsqueeze(2).to_broadcast([128, 16, 64]), op=ALU.mult), r=[H, E3], w=[H])
        if not LATE:
            late_dve(1)
            late_dve(0)
        cbk = B[0]
        for g in range(4):
            S.op("pe", lambda e, g=g: e.matmul(cbk[:, g * 128:(g + 1) * 128], bmT(g), cmT(g), start=True, stop=True),
                 r=rT, w=[cbk], inc=(g == 3))
        if hook is not None:
            hook(2)
        def stageA(g):
            seg = B[1 + g % 2]
            Eb = Ebuf[g % 2]
            Mb = Mbuf[g % 2]
            for hl in range(4):
                h = g * 4 + hl
                o = seg[:, hl * 128:(hl + 1) * 128]
                S.op("pe", lambda e, o=o, h=h: e.matmul(o, LAH[:, ch, c0 + h:c0 + h + 1].to_broadcast([128, 128]),
                                                        TRI[d][0], start=True, stop=False),
                     r=[LAH, CB_], w=[seg], inc=False)
                S.op("pe", lambda e, o=o, h=h: e.matmul(o, LAL[:, ch, c0 + h:c0 + h + 1].to_broadcast([128, 128]),
                                                        TRI[d][0], start=False, stop=False),
                     r=[LAL, CB_], w=[seg], inc=False)
                S.op("pe", lambda e, o=o: e.matmul(o, IDB, NEGM[d], start=False, stop=True),
                     r=[CB_], w=[seg], inc=(hl == 3))
            for hl in range(4):
                h = g * 4 + hl
                S.op("act", lambda e, hl=hl, h=h: e.activation(
                    out=Eb[:, hl * 128:(hl + 1) * 128], in_=seg[:, hl * 128:(hl + 1) * 128], func=AF.Exp,
                    bias=nacs[:, h:h + 1], scale=1.0), r=[seg, nacs], w=[Eb])
            S.op("dve", lambda e: e.tensor_tensor(
                out=Mb[:, :].rearrange("p (h q) -> p h q", h=4), in0=Eb[:, :].rearrange("p (h q) -> p h q", h=4),
                in1=cbk[:, g * 128:(g + 1) * 128].unsqueeze(1).to_broadcast([128, 4, 128]), op=ALU.mult),
                r=[Eb, cbk], w=[Mb])

        def stageB(g):
            Yg = B[3 + g % 2]
            Mb = Mbuf[g % 2]
            if xsD is not None:
                S.op("pe", lambda e: e.matmul(Yg[:, 0:256], IDB, xsD[:, g * 256:(g + 1) * 256], start=True, stop=False),
                     r=[CB_, xsDT], w=[Yg], inc=False)
            if ypacc is not None:
                S.op("pe", lambda e: e.matmul(Yg[:, 0:256], IDF, ypacc[:, g * 256:(g + 1) * 256], start=True, stop=False),
                     r=[CF, ypacc], w=[Yg], inc=False)
            for hl in range(4):
                h = g * 4 + hl
                S.op("pe", lambda e, hl=hl, h=h: e.matmul(
                    Yg[:, hl * 64:(hl + 1) * 64], Mb[:, hl * 128:(hl + 1) * 128], xd[:, h * 64:(h + 1) * 64],
                    start=(xsD is None and ypacc is None), stop=True), r=[Mb, xd], w=[Yg], inc=False)
            S.op("pe", lambda e: e.matmul(Yg[:, 256:512], cmT(g), Hb[:, g * 256:(g + 1) * 256],
                                          start=True, stop=True), r=rT + [Hb], w=[Yg], inc=True)
            t1 = t1s[g % 2]
            S.op("dve", lambda e: e.tensor_tensor(
                out=t1[:, :].rearrange("p (h q) -> p h q", h=4), in0=Yg[:, 256:512].rearrange("p (h q) -> p h q", h=4),
                in1=E3[:, g * 4:(g + 1) * 4].unsqueeze(2).to_broadcast([128, 4, 64]), op=ALU.mult),
                r=[Yg, E3], w=[t1])
            yfin(g, Yg, t1)

        stageA(0)
        for g in range(4):
            if g + 1 < 4:
                stageA(g + 1)
            if g < 2 and LATE:
                late_dve(g)
            if hook is not None:
                hook(1)
            stageB(g)
        for g in range(4):
            sgk = B[5]
            S.op("pe", lambda e, g=g, sgk=sgk: e.matmul(sgk[:, 0:256], bm_tok[:, g * 128:(g + 1) * 128],
                                                        xdd[:, g * 256:(g + 1) * 256], start=True, stop=True),
                 r=rT + [xdd], w=[sgk], inc=True)
            Hg = H[:, g * 256:(g + 1) * 256]
            S.op("dve", lambda e, Hg=Hg, sgk=sgk: e.tensor_tensor(out=Hg, in0=Hg, in1=sgk[:, 0:256], op=ALU.add),
                 r=[H, sgk], w=[H])
            S.op("act", lambda e, g=g, Hg=Hg: e.activation(out=Hb[:, g * 256:(g + 1) * 256], in_=Hg, func=AF.Copy),
                 r=[H], w=[Hb])
            if hook is not None:
                hook(1)

    def ssd_work(ph):
        return dict(E3=sb([128, 48], F32, ph), nacs=sb([128, 16], F32, ph), wd=sb([128, 16], F32, ph),
                    xd=sb([128, D], BF16, ph), xdd=sb([128, D], BF16, ph),
                    E=[sb([128, 512], F32, ph) for _ in range(2)], M=[sb([128, 512], BF16, ph) for _ in range(2)],
                    t1=[sb([128, 256], F32, ph) for _ in range(2)])

    def init_state(kind, d, H, Hb, stg):
        if kind == "p":
            S.op("pool", lambda e: e.memset(H[:], 0.0), w=[H])
            S.op("pool", lambda e: e.memset(Hb[:], 0.0), w=[Hb])
            return
        S.dma("sp", stg[:], state0[d].rearrange("(k p) n -> p k n", p=128), w=[stg])
        for half in range(2):
            bk = B[3 + half]
            for kk in range(4):
                k = half * 4 + kk
                S.op("pe", lambda e, k=k, kk=kk, bk=bk: e.matmul(bk[:, kk * 128:(kk + 1) * 128], stg[:, k, :], IDF,
                                                                 start=True, stop=True), r=[stg, CF], w=[bk], inc=(kk == 3))
            S.op("act", lambda e, half=half, bk=bk: e.activation(out=H[:, half * 512:(half + 1) * 512], in_=bk[:, :],
                                                                 func=AF.Copy), r=[bk], w=[H])
        S.op("act", lambda e: e.activation(out=Hb[:], in_=H[:], func=AF.Copy), r=[H], w=[Hb])

    def final_state(pidx, d, H, stg):
        for half in range(2):
            for kk in range(4):
                k = half * 4 + kk
                bk = B[3 + kk % 2]
                S.op("pe", lambda e, k=k, bk=bk: e.matmul(bk[:, 0:128], H[:, k * 128:(k + 1) * 128], IDF,
                                                          start=True, stop=True), r=[H, CF], w=[bk], inc=True)
                S.op("act", lambda e, k=k, bk=bk: e.activation(out=stg[:, k, :], in_=bk[:, 0:128], func=AF.Copy),
                     r=[bk], w=[stg])
        S.dma("sp", st_out[pidx, d].rearrange("(k p) n -> p k n", p=128), stg[:], r=[stg])

    with contextlib.ExitStack() as ph:
        CW = sb([128, 16, 5], F32, ph)
        CBI = sb([128, 16], F32, ph)
        DBC = sb([128, D], F32, ph)
        S.dma("sp", CW[:], cwT, w=[CW])
        S.dma("sp", CBI[:], cbT, w=[CBI])
        S.dma("sp", DBC[:], bc(drow), w=[DBC])
        wins = [sb([128, 16, 516], BF16, ph) for _ in range(2)]
        DG5 = [sb([128, 5, 128], BF16, ph) for _ in range(2)]
        XF = [sb([128, 16, 512], BF16, ph) for _ in range(2)]
        toks = [sb([128, 1536], BF16, ph) for _ in range(2)]
        yps = [sb([128, D], F32, ph) for _ in range(2)]
        xsds = [sb([128, D], BF16, ph) for _ in range(2)]
        Wk = ssd_work(ph)
        H = sb([128, D], F32, ph)
        Hb = sb([128, D], BF16, ph)
        stg = sb([128, 8, 128], F32, ph)
        cn = 0
        pidx = 0
        tiles3 = [(s, kind, L, min(512, L), t0) for s, (kind, L) in enumerate(seqs) for t0 in range(0, L, min(512, L))]
        pend = []
        dgn = [0]

        def pump(n=1):
            for _ in range(n):
                if pend:
                    pend.pop(0)()

        def queue_conv(ti):
            s, kind, L, TT, t0 = tiles3[ti]
            win = wins[ti % 2]
            xf = XF[ti % 2]
            p0 = ptok0[s] + t0

            def ld():
                S.dma("sp", win[:, :, 0:TT + 4], XBCR.rearrange("(j p) t -> p j t", p=128)[:, :, p0 - 2:p0 + TT + 2], w=[win])
            pend.append(ld)
            for j in range(16):
                def job(j=j):
                    dg = DG5[dgn[0] % 2]
                    dgn[0] += 1
                    bk = B[6]
                    S.op("dve", lambda e: e.tensor_tensor(
                        out=dg[:, :, :], in0=IDB.unsqueeze(1).to_broadcast([128, 5, 128]),
                        in1=CW[:, j, :].unsqueeze(2).to_broadcast([128, 5, 128]), op=ALU.mult), r=[CB_, CW], w=[dg])
                    for k in range(5):
                        S.op("pe", lambda e, k=k: e.matmul(bk[:, 0:TT], dg[:, k, :], win[:, j, k:k + TT],
                                                           start=(k == 0), stop=(k == 4)), r=[dg, win], w=[bk], inc=(k == 4))
                    S.op("act", lambda e: e.activation(out=xf[:, j, 0:TT], in_=bk[:, 0:TT], func=AF.Silu,
                                                       bias=CBI[:, j:j + 1], scale=1.0), r=[bk, CBI], w=[xf])
                pend.append(job)

        queue_conv(0)
        pump(len(pend))
        prev_s = -1
        for ti, (s, kind, L, TT, t0) in enumerate(tiles3):
            nb = TT // 128
            if ti + 1 < len(tiles3):
                queue_conv(ti + 1)
                pump(len(pend))
            if s != prev_s:
                init_state(kind, 0, H, Hb, stg)
                prev_s = s
            if True:
                xf = XF[ti % 2]
                for blk in range(nb):
                    ch = (tok0[s] + t0) // 128 + blk
                    tk = toks[cn % 2]
                    yp = yps[cn % 2]
                    cn += 1
                    sl = slice(blk * 128, (blk + 1) * 128)
                    for j in range(8):
                        S.op("pe", lambda e, j=j, sl=sl: e.transpose(out=TB[:, j * 128:(j + 1) * 128], in_=xf[:, j, sl],
                                                                     identity=IDB), r=[xf, CB_], w=[TB], inc=(j == 7))
                    S.op("act", lambda e, tk=tk: e.activation(out=tk[:, 0:1024], in_=TB[:, 0:1024], func=AF.Copy),
                         r=[TB], w=[tk])
                    for j in range(4):
                        S.op("pe", lambda e, j=j, sl=sl: e.transpose(out=TB[:, j * 128:(j + 1) * 128], in_=xf[:, 8 + j, sl],
                                                                     identity=IDB), r=[xf, CB_], w=[TB], inc=(j == 3))
                    S.op("act", lambda e, tk=tk: e.activation(out=tk[:, 1024:1536], in_=TB[:, 0:512], func=AF.Copy),
                         r=[TB], w=[tk])
                    S.dma("sp", CTS[ch][:, 0:1536], tk[:, :], r=[tk])
                    S.dma("sp", CTS[ch][:, 1536:2560].rearrange("p (j t) -> p j t", j=8), xf[:, 8:16, sl], r=[xf])

                    xsd = xsds[cn % 2]
                    S.op("pool", lambda e, tk=tk, xsd=xsd: e.tensor_tensor(out=xsd[:], in0=tk[:, 0:1024], in1=DBC[:],
                                                                           op=ALU.mult), r=[tk, DBC], w=[xsd])

                    def yfin(g, Yg, t1, tk=tk, yp=yp):
                        gs = slice(g * 256, (g + 1) * 256)
                        S.op("dve", lambda e: e.tensor_tensor(out=yp[:, gs], in0=Yg[:, 0:256], in1=t1[:], op=ALU.add),
                             r=[Yg, t1], w=[yp])

                    ssd_chunk(0, ch, tk[:, 0:1024], tk[:, 1024:1536],
                              lambda g, sl=sl: xf[:, 8 + g, sl], lambda g, sl=sl: xf[:, 12 + g, sl],
                              [tk, xf], H, Hb, Wk, yfin, xsD=xsd[:, :], xsDT=xsd)
                    g0 = ch * 128
                    S.dma("sp", YP[g0:g0 + 128, :], yp[:], r=[yp])
            if kind == "p" and t0 + TT >= L:
                final_state(pidx, 0, H, stg)
                pidx += 1
        S.barrier()

    if upto == "k3":
        return finish()

    def ffn(layer, src, dst, final):
        with contextlib.ExitStack() as ph:
            WUP = sb([128, 8, 2 * DFF], BF16, ph)
            WLu = WLoad(WUP, w_up[layer], 8, 2 * DFF, [0, 1408, 2816, 4224, 5632])
            WLu.need(0)
            CT5 = CondTiles(ph, [(layer, 4), (layer, 3)])
            xts = [sb([128, D], F32, ph) for _ in range(2)]
            nm = (sb([128, D], BF16, ph), sb([128, 4], F32, ph), sb([128, D], F32, ph))
            hmb = [sb([128, D], BF16, ph) for _ in range(2)]
            hTs = [sb([128, 8, 512], BF16, ph) for _ in range(2)]
            ust = [sb([128, 4, 512], BF16, ph) for _ in range(2)]
            xc = [0]
            un = 0
            tiles5 = [(s, kind, L, min(512, L), t0) for s, (kind, L) in enumerate(seqs) for t0 in range(0, L, min(512, L))]
            pend = []

            def pump(n=1):
                for _ in range(n):
                    if pend:
                        pend.pop(0)()

            def queue_tile(ti):
                s, kind, L, TT, t0 = tiles5[ti]
                pend.append(lambda: CT5.set(cond_of[s]))
                pend.extend(hT_jobs(src, s, t0, TT, CT5.t[0], CT5.t[1], xts, nm, hmb, hTs[ti % 2], xc))

            queue_tile(0)
            for ti, (s, kind, L, TT, t0) in enumerate(tiles5):
                if True:
                    pump(len(pend))
                    hT = hTs[ti % 2]
                    if ti + 1 < len(tiles5):
                        queue_tile(ti + 1)
                    p0 = ptok0[s] + t0
                    for jg in range(11):
                        if 1 <= jg <= 9:
                            pump(2 if jg == 1 else 1)
                        us_ = ust[un % 2]
                        un += 1
                        WLu.pump(3)
                        for jj in range(4):
                            j = jg * 4 + jj
                            bk = B[j % 4]
                            WLu.need(j // 11)
                            for k in range(8):
                                S.op("pe", lambda e, k=k, j=j, bk=bk: e.matmul(
                                    bk[:, 0:TT], WUP[:, k, j * 128:(j + 1) * 128], hT[:, k, 0:TT],
                                    start=(k == 0), stop=(k == 7)), r=[hT, WLu.rb(j * 128)], w=[bk], inc=(k == 7))
                            if j % 2 == 0:
                                S.op("act", lambda e, jj=jj, bk=bk, us_=us_: e.activation(
                                    out=us_[:, jj, 0:TT], in_=bk[:, 0:TT], func=AF.Copy), r=[bk], w=[us_])
                            else:
                                S.op("dve", lambda e, jj=jj, bk=bk, us_=us_: e.tensor_copy(
                                    out=us_[:, jj, 0:TT], in_=bk[:, 0:TT]), r=[bk], w=[us_])
                        S.dma("sp", UR.rearrange("(j p) t -> p j t", p=128)[:, jg * 4:(jg + 1) * 4, p0:p0 + TT],
                              us_[:, :, 0:TT], r=[us_])
            S.barrier()
        with contextlib.ExitStack() as ph:
            WDN = sb([128, 22, D], BF16, ph)
            WLd = WLoad(WDN, w_down[layer], 22, D)
            FW = sb([128, 22, 18], F32, ph)
            S.dma("sp", FW[:], fcwT[layer], w=[FW])
            CT6 = CondTiles(ph, [(layer, 5)])
            GT2 = CT6.t[0]
            if final:
                NFB = sb([128, D], F32, ph)
                S.dma("sp", NFB[:], bc(nf_row), w=[NFB])
                fin = (sb([128, D], BF16, ph), sb([128, 4], F32, ph))
            wins = [sb([128, 2, 640], BF16, ph) for _ in range(3)]
            DG = [sb([128, 18, 128], BF16, ph) for _ in range(2)]
            sgs = [sb([128, 512], F32, ph) for _ in range(2)]
            aTs = [sb([128, 22, 512], BF16, ph) for _ in range(2)]
            xts = [sb([128, D], F32, ph) for _ in range(2)]
            tts = [sb([128, 512], F32, ph) for _ in range(2)]
            xos = [sb([128, D], F32, ph) for _ in range(2)]
            URv = UR.rearrange("(v j p) t -> p v j t", v=2, p=128)
            xn = [0]
            items = []
            tcount = 0
            for s, (kind, L) in enumerate(seqs):
                TT = min(512, L)
                for t0 in range(0, L, TT):
                    for jj in range(22):
                        items.append((s, kind, L, TT, t0, jj, tcount))
                    tcount += 1

            def taps_of(kind):
                taps = [(dy, dx) for dy in range(3) for dx in range(3)] if kind == "s" else [(1, dx) for dx in range(3)]
                return [tp for tp in taps if tp[1] == 1] + [tp for tp in taps if tp[1] != 1]

            def prep(i):
                s, kind, L, TT, t0, jj, tc = items[i]
                win = wins[i % 3]
                dg = DG[i % 2]
                p0 = ptok0[s] + t0
                if kind == "s":
                    S.dma("sp", win[:, :, 0:TT + 128], URv[:, :, jj, p0 - 64:p0 + TT + 64], w=[win])
                else:
                    S.dma("sp", win[:, :, 0:TT + 2], URv[:, :, jj, p0 - 1:p0 + TT + 1], w=[win])
                S.op("dve", lambda e: e.tensor_tensor(
                    out=dg[:, :, :], in0=IDB.unsqueeze(1).to_broadcast([128, 18, 128]),
                    in1=FW[:, jj, :].unsqueeze(2).to_broadcast([128, 18, 128]), op=ALU.mult),
                    r=[CB_, FW], w=[dg])

            def run_item(i):
                s, kind, L, TT, t0, jj, tc = items[i]
                win = wins[i % 3]
                dg = DG[i % 2]
                sgt = sgs[i % 2]
                bv, bg = B[2 * (i % 2)], B[2 * (i % 2) + 1]
                aT = aTs[tc % 2]
                order = taps_of(kind)
                for (v, bk) in ((0, bv), (1, bg)):
                    for tj, (dy, dx) in enumerate(order):
                        ti = ORDER9.index((dy, dx))
                        if kind == "s":
                            w3 = win[:, v, dy * 64:dy * 64 + TT].rearrange("p (r c) -> p r c", c=64)
                            b3 = bk[:, 0:TT].rearrange("p (r c) -> p r c", c=64)
                            if dx == 1:
                                o_ap, i_ap = bk[:, 0:TT], win[:, v, dy * 64:dy * 64 + TT]
                            elif dx == 0:
                                o_ap, i_ap = b3[:, :, 1:64], w3[:, :, 0:63]
                            else:
                                o_ap, i_ap = b3[:, :, 0:63], w3[:, :, 1:64]
                        else:
                            o_ap, i_ap = bk[:, 0:TT], win[:, v, dx:dx + TT]
                        last = tj == len(order) - 1
                        S.op("pe", lambda e, o_ap=o_ap, i_ap=i_ap, v=v, ti=ti, tj=tj, last=last: e.matmul(
                            o_ap, dg[:, v * 9 + ti, :], i_ap, start=(tj == 0), stop=last),
                            r=[dg, win], w=[bk], inc=last)
                S.op("act", lambda e: e.activation(out=sgt[:, 0:TT], in_=bg[:, 0:TT], func=AF.Silu), r=[bg], w=[sgt])
                S.op("dve", lambda e: e.tensor_tensor(
                    out=aT[:, jj, 0:TT], in0=bv[:, 0:TT], in1=sgt[:, 0:TT], op=ALU.mult), r=[bv, sgt], w=[aT])
                WLd.pump(1)
                if jj != 21:
                    return
                WLd.all()
                CT6.set(cond_of[s])
                for blk in range(TT // 128):
                    g0 = tok0[s] + t0 + blk * 128
                    xt = xts[xn[0] % 2]
                    xo = xos[xn[0] % 2]
                    xn[0] += 1
                    S.dma("sp", xt[:], src[g0:g0 + 128, :], w=[xt])
                    for half in range(2):
                        bk = B[4 + half]
                        hs = slice(half * 512, (half + 1) * 512)
                        for k in range(22):
                            S.op("pe", lambda e, k=k, blk=blk, hs=hs, bk=bk: e.matmul(
                                bk[:, :], aT[:, k, blk * 128:(blk + 1) * 128], WDN[:, k, hs],
                                start=(k == 0), stop=(k == 21)), r=[aT, WLd.rb(0)], w=[bk], inc=(k == 21))
                        tt = tts[half]
                        S.op("dve", lambda e, hs=hs, bk=bk, tt=tt: e.tensor_tensor(
                            out=tt[:], in0=bk[:, :], in1=GT2[:, hs], op=ALU.mult), r=[bk, GT2], w=[tt])
                        S.op("dve", lambda e, hs=hs, tt=tt, xt=xt, xo=xo: e.tensor_tensor(
                            out=xo[:, hs], in0=tt[:], in1=xt[:, hs], op=ALU.add), r=[tt, xt], w=[xo])
                    if final:
                        junk, ss = fin
                        S.op("act", lambda e, xo=xo: e.activation(out=junk[:], in_=xo[:], func=AF.Square,
                                                                  accum_out=ss[:, 0:1]), r=[xo], w=[junk, ss])
                        S.op("act", lambda e: e.activation(out=ss[:, 1:2], in_=ss[:, 0:1], func=AF.Sqrt, bias=EPSC[:, 0:1], scale=1.0 / D), r=[ss, EPSC], w=[ss])
                        S.op("dve", lambda e: e.reciprocal(out=ss[:, 2:3], in_=ss[:, 1:2]), r=[ss], w=[ss])
                        S.op("dve", lambda e, xo=xo: e.scalar_tensor_tensor(
                            out=xo[:], in0=xo[:], scalar=ss[:, 2:3], in1=NFB[:], op0=ALU.mult, op1=ALU.mult),
                            r=[xo, ss, NFB], w=[xo])
                    S.dma("sp", dst[g0:g0 + 128, :], xo[:], r=[xo])

            prep(0)
            for i in range(len(items)):
                if i + 1 < len(items):
                    prep(i + 1)
                run_item(i)
            S.barrier()

    with contextlib.ExitStack() as ph:
        WO = sb([128, 16, D], BF16, ph)
        WLo = WLoad(WO, w_out, 16, D)
        CCW = sb([128, 8, 31], F32, ph)
        CCB = sb([128, 8], F32, ph)
        LNG = sb([128, 8], F32, ph)
        LNB = sb([128, 8], F32, ph)
        NGB = sb([128, D], F32, ph)
        S.dma("sp", CCW[:], ccwT, w=[CCW])
        S.dma("sp", CCB[:], ccbT, w=[CCB])
        S.dma("sp", LNG[:], lngT, w=[LNG])
        S.dma("sp", LNB[:], lnbT, w=[LNB])
        S.dma("sp", NGB[:], bc(ngrow), w=[NGB])
        CT4 = CondTiles(ph, [(0, 2)])
        GT1 = CT4.t[0]
        cts = [sb([128, 2560], BF16, ph) for _ in range(2)]
        yps = [sb([128, D], F32, ph) for _ in range(2)]
        zts = [sb([128, D], F32, ph) for _ in range(1)]
        Wk = ssd_work(ph)
        H = sb([128, D], F32, ph)
        Hb = sb([128, D], BF16, ph)
        stg = sb([128, 8, 128], F32, ph)
        junk = sb([128, D], BF16, ph)
        ss = sb([128, 4], F32, ph)
        ynb = sb([128, D], BF16, ph)
        ysT = [sb([128, 8, 512], BF16, ph) for _ in range(1)]
        uT = [sb([128, 8, 512], BF16, ph) for _ in range(1)]
        gwin = sb([128, 8, 542], BF16, ph)
        DG31 = [sb([128, 31, 128], BF16, ph) for _ in range(2)]
        cacc = [sb([128, 512], F32, ph) for _ in range(8)]
        vb = [sb([128, 512], BF16, ph) for _ in range(2)]
        qb = [sb([128, 512], BF16, ph) for _ in range(2)]
        mean_sb = sb([128, 512], F32, ph)
        rstd_sb = sb([128, 512], F32, ph)
        xts = [sb([128, D], F32, ph) for _ in range(1)]
        tts = [sb([128, 512], F32, ph) for _ in range(2)]
        xos = [sb([128, D], F32, ph) for _ in range(1)]
        cn = 0
        xn = 0
        pidx = 0
        for s, (kind, L) in enumerate(seqs):
            CT4.set(cond_of[s])
            TT = min(512, L)
            nb = TT // 128
            init_state(kind, 1, H, Hb, stg)
            for t0 in range(L - TT, -1, -TT):
                yT = ysT[0]
                pend4 = []
                p0 = ptok0[s] + t0

                def ld4(p0=p0, TT=TT):
                    S.dma("sp", gwin[:, :, 0:TT + 30], GLU.rearrange("(j p) t -> p j t", p=128)[:, :, p0 - 15:p0 + TT + 15],
                          w=[gwin])
                pend4.append(ld4)
                for j in range(8):
                    for (k0, k1) in ((0, 8), (8, 16), (16, 24), (24, 31)):
                        def cjob(j=j, TT=TT, k0=k0, k1=k1):
                            dg = DG31[j % 2]
                            cbk_ = B[6]
                            acc = cacc[j]
                            if k0 == 0:
                                S.op("dve", lambda e: e.tensor_tensor(
                                    out=dg[:, :, :], in0=IDB.unsqueeze(1).to_broadcast([128, 31, 128]),
                                    in1=CCW[:, j, :].unsqueeze(2).to_broadcast([128, 31, 128]), op=ALU.mult),
                                    r=[CB_, CCW], w=[dg])
                            for k in range(k0, k1):
                                S.op("pe", lambda e, k=k: e.matmul(
                                    cbk_[:, 0:TT], dg[:, k, :], gwin[:, j, k:k + TT], start=(k == 0), stop=(k == 30)),
                                    r=[dg, gwin], w=[cbk_], inc=(k == 30))
                            if k1 == 31:
                                S.op("act", lambda e: e.activation(
                                    out=acc[:, 0:TT], in_=cbk_[:, 0:TT], func=AF.Identity, bias=CCB[:, j:j + 1], scale=1.0),
                                    r=[cbk_, CCB], w=[acc])
                        pend4.append(cjob)
                        pend4.append(lambda: WLo.pump(1))

                def pump4(n=1):
                    for _ in range(n):
                        if pend4:
                            pend4.pop(0)()
                pump4(1)
                def ln_part(TT=TT):
                    pump4(len(pend4))
                    mb, qbk = B[0], B[1]
                    for j in range(8):
                        acc = cacc[j]
                        v_, q_ = vb[j % 2], qb[j % 2]
                        S.op("act", lambda e, acc=acc, v_=v_: e.activation(out=v_[:, 0:TT], in_=acc[:, 0:TT], func=AF.Copy),
                             r=[acc], w=[v_])
                        S.op("act", lambda e, acc=acc, q_=q_: e.activation(out=q_[:, 0:TT], in_=acc[:, 0:TT], func=AF.Square),
                             r=[acc], w=[q_])
                        S.op("pe", lambda e, j=j, v_=v_: e.matmul(mb[:, 0:TT], ONESM, v_[:, 0:TT], start=(j == 0), stop=(j == 7)),
                             r=[CB_, v_], w=[mb], inc=True)
                        S.op("pe", lambda e, j=j, q_=q_: e.matmul(qbk[:, 0:TT], ONESM, q_[:, 0:TT], start=(j == 0), stop=(j == 7)),
                             r=[CB_, q_], w=[qbk], inc=True)
                    S.op("act", lambda e: e.activation(out=mean_sb[:, 0:TT], in_=mb[:, 0:TT], func=AF.Copy), r=[mb], w=[mean_sb])
                    S.op("dve", lambda e: e.tensor_tensor(out=rstd_sb[:, 0:TT], in0=mean_sb[:, 0:TT], in1=mean_sb[:, 0:TT],
                                                          op=ALU.mult), r=[mean_sb], w=[rstd_sb])
                    S.op("dve", lambda e: e.tensor_tensor(out=rstd_sb[:, 0:TT], in0=qbk[:, 0:TT], in1=rstd_sb[:, 0:TT],
                                                          op=ALU.subtract), r=[qbk, rstd_sb], w=[rstd_sb])
                    S.op("act", lambda e: e.activation(out=rstd_sb[:, 0:TT], in_=rstd_sb[:, 0:TT], func=AF.Sqrt, bias=EPSC[:, 0:1],
                                                       scale=1.0), r=[rstd_sb, EPSC], w=[rstd_sb])
                    S.op("dve", lambda e: e.reciprocal(out=rstd_sb[:, 0:TT], in_=rstd_sb[:, 0:TT]), r=[rstd_sb], w=[rstd_sb])
                    u_ = uT[0]
                    for j in range(8):
                        eng = "dve"
                        acc = cacc[j]
                        S.op(eng, lambda e, acc=acc: e.tensor_tensor(out=acc[:, 0:TT], in0=acc[:, 0:TT], in1=mean_sb[:, 0:TT],
                                                                     op=ALU.subtract), r=[acc, mean_sb], w=[acc])
                        S.op(eng, lambda e, acc=acc: e.tensor_tensor(out=acc[:, 0:TT], in0=acc[:, 0:TT], in1=rstd_sb[:, 0:TT],
                                                                     op=ALU.mult), r=[acc, rstd_sb], w=[acc])
                        S.op("act", lambda e, j=j, acc=acc: e.activation(out=u_[:, j, 0:TT], in_=acc[:, 0:TT], func=AF.Silu,
                                                                         bias=LNB[:, j:j + 1], scale=LNG[:, j:j + 1]),
                             r=[acc, LNB, LNG], w=[u_])

                for blk in range(nb - 1, -1, -1):
                    if blk == 0 and nb > 1:
                        ln_part()
                    ch = (tok0[s] + t0) // 128 + blk
                    ct = cts[cn % 2]
                    yp = yps[cn % 2]
                    zt = zts[0]
                    cn += 1
                    g0 = ch * 128
                    S.dma("sp", ct[:], CTS[ch], w=[ct])
                    S.dma("sp", yp[:], YP[g0:g0 + 128, :], w=[yp])
                    S.dma("sp", zt[:], ZS[g0:g0 + 128, :], w=[zt])

                    def yfin(g, Yg, t1, yp=yp):
                        gs = slice(g * 256, (g + 1) * 256)
                        S.op("dve", lambda e: e.tensor_tensor(out=yp[:, gs], in0=Yg[:, 0:256], in1=t1[:], op=ALU.add),
                             r=[Yg, t1], w=[yp])

                    ssd_chunk(1, ch, ct[:, 0:1024], ct[:, 1024:1536],
                              lambda g, ct=ct: ct[:, 1536 + g * 128:1536 + (g + 1) * 128],
                              lambda g, ct=ct: ct[:, 2048 + g * 128:2048 + (g + 1) * 128],
                              [ct], H, Hb, Wk, yfin, hook=pump4, ypacc=yp)
                    S.op("dve", lambda e, yp=yp, zt=zt: e.tensor_tensor(out=yp[:], in0=yp[:], in1=zt[:], op=ALU.mult),
                         r=[yp, zt], w=[yp])
                    S.op("act", lambda e, yp=yp: e.activation(out=junk[:], in_=yp[:], func=AF.Square, accum_out=ss[:, 0:1]),
                         r=[yp], w=[junk, ss])
                    S.op("act", lambda e: e.activation(out=ss[:, 1:2], in_=ss[:, 0:1], func=AF.Sqrt, bias=EPSC[:, 0:1], scale=1.0 / D), r=[ss, EPSC], w=[ss])
                    S.op("dve", lambda e: e.reciprocal(out=ss[:, 2:3], in_=ss[:, 1:2]), r=[ss], w=[ss])
                    S.op("dve", lambda e, yp=yp: e.scalar_tensor_tensor(out=ynb[:], in0=yp[:], scalar=ss[:, 2:3], in1=NGB[:],
                                                                        op0=ALU.mult, op1=ALU.mult), r=[yp, ss, NGB], w=[ynb])
                    for k in range(8):
                        S.op("pe", lambda e, k=k: e.transpose(out=TB[:, k * 128:(k + 1) * 128],
                                                              in_=ynb[:, k * 128:(k + 1) * 128], identity=IDB),
                             r=[ynb, CB_], w=[TB], inc=(k == 7))
                    S.op("act", lambda e, blk=blk: e.activation(
                        out=yT[:, :, blk * 128:(blk + 1) * 128], in_=TB[:, :].rearrange("p (k t) -> p k t", k=8),
                        func=AF.Copy), r=[TB], w=[yT])
                u_ = uT[0]
                if nb == 1:
                    ln_part()
                WLo.all()
                for blk in range(nb):
                    g0 = tok0[s] + t0 + blk * 128
                    xt = xts[0]
                    xo = xos[0]
                    xn += 1
                    S.dma("sp", xt[:], x_in[g0:g0 + 128, :], w=[xt])
                    for half in range(2):
                        bk = B[3 + half]
                        hs = slice(half * 512, (half + 1) * 512)
                        for k in range(16):
                            lhs = yT[:, k, blk * 128:(blk + 1) * 128] if k < 8 else u_[:, k - 8, blk * 128:(blk + 1) * 128]
                            S.op("pe", lambda e, k=k, lhs=lhs, hs=hs, bk=bk: e.matmul(
                                bk[:, :], lhs, WO[:, k, hs], start=(k == 0), stop=(k == 15)),
                                r=[yT, u_, WLo.rb(0)], w=[bk], inc=(k == 15))
                        tt = tts[half]
                        S.op("dve", lambda e, hs=hs, bk=bk, tt=tt: e.tensor_tensor(
                            out=tt[:], in0=bk[:, :], in1=GT1[:, hs], op=ALU.mult), r=[bk, GT1], w=[tt])
                        S.op("pool", lambda e, hs=hs, tt=tt, xt=xt, xo=xo: e.tensor_tensor(
                            out=xo[:, hs], in0=tt[:], in1=xt[:, hs], op=ALU.add), r=[tt, xt], w=[xo])
                    S.dma("sp", XA[g0:g0 + 128, :], xo[:], r=[xo])
            if kind == "p":
                final_state(pidx, 1, H, stg)
                pidx += 1
        S.barrier()

    if upto == "k4":
        return finish()
    ffn(0, XA, XB, False)
    if upto == "k6":
        return finish()

    with contextlib.ExitStack() as ph:
        FWB = sb([128, 8, D], BF16, ph)
        WLf = WLoad(FWB, fnet_w, 8, D)
        FBB = sb([128, D], F32, ph)
        S.dma("sp", FBB[:], bc(fnet_b), w=[FBB])
        CT7 = CondTiles(ph, [(1, 2)])
        GT1 = CT7.t[0]
        Lmax = max(Ls)
        KCm = Lmax // 128
        HM = [sb([128, D], BF16, ph) for _ in range(KCm)]
        xc = [0]
        ln = 0
        jn = 0
        xn = 0
        for s, (kind, L) in enumerate(seqs):
            CT7.set(cond_of[s])
            KC = L // 128
            with contextlib.ExitStack() as pha:
                CT7a = CondTiles(pha, [(1, 1), (1, 0)])
                CT7a.set(cond_of[s])
                xtsa = [sb([128, D], F32, pha) for _ in range(2)]
                nm = (sb([128, D], BF16, pha), sb([128, 4], F32, pha), sb([128, D], F32, pha))
                for kc in range(KC):
                    make_hT(pha, XB, s, kc * 128, 128, CT7a.t[0], CT7a.t[1], xtsa, nm, HM[kc], None, xc)
                S.barrier()
            WLf.all()
            phb = contextlib.ExitStack()
            CLt = [sb([128, KC, 256], BF16, phb) for _ in range(2)]
            SLt = [sb([128, KC, 256], BF16, phb) for _ in range(2)]
            pcs = [sb([128, 512], BF16, phb) for _ in range(2)]
            uTt = [sb([128, 8, 256], BF16, phb) for _ in range(2)]
            tts = [sb([128, 512], F32, phb) for _ in range(2)]
            xos = [sb([128, D], F32, phb) for _ in range(1)]
            xts = [sb([128, D], F32, phb) for _ in range(1)]
            scale = 1.0 / float(np.sqrt(L * 128.0))
            NLT = L // 256

            def ld_dft(lt):
                S.dma("sp", CLt[lt % 2][:, :, :], dftL[L][0][lt], w=[CLt[lt % 2]])
                S.dma("sp", SLt[lt % 2][:, :, :], dftL[L][1][lt], w=[SLt[lt % 2]])
            ld_dft(0)
            for lt in range(NLT):
                cl, sl_ = CLt[lt % 2], SLt[lt % 2]
                uT_ = uTt[ln % 2]
                ln += 1
                if lt + 1 < NLT:
                    ld_dft(lt + 1)
                for j in range(8):
                    bk = B[jn % 2]
                    pc = pcs[jn % 2]
                    ub = B[2 + jn % 2]
                    jn += 1
                    for (o, mat) in ((0, cl), (256, sl_)):
                        for kc in range(KC):
                            S.op("pe", lambda e, o=o, mat=mat, kc=kc, j=j, bk=bk: e.matmul(
                                bk[:, o:o + 256], HM[kc][:, j * 128:(j + 1) * 128], mat[:, kc, :],
                                start=(kc == 0), stop=(kc == KC - 1)), r=[HM[kc], mat], w=[bk], inc=(kc == KC - 1))
                    S.op("act", lambda e, bk=bk, pc=pc: e.activation(out=pc[:], in_=bk[:, :], func=AF.Copy), r=[bk], w=[pc])
                    S.op("pe", lambda e, pc=pc, ub=ub: e.matmul(ub[:, 0:256], CCS[:, 0, :], pc[:, 0:256], start=True, stop=False),
                         r=[CCS, pc], w=[ub], inc=False)
                    S.op("pe", lambda e, pc=pc, ub=ub: e.matmul(ub[:, 0:256], CCS[:, 1, :], pc[:, 256:512], start=False, stop=True),
                         r=[CCS, pc], w=[ub], inc=True)
                    S.op("act", lambda e, j=j, ub=ub: e.activation(out=uT_[:, j, :], in_=ub[:, 0:256], func=AF.Copy, scale=scale),
                         r=[ub], w=[uT_])
                for blk in range(2):
                    g0 = tok0[s] + lt * 256 + blk * 128
                    xt = xts[0]
                    xo = xos[0]
                    xn += 1
                    S.dma("sp", xt[:], XB[g0:g0 + 128, :], w=[xt])
                    for half in range(2):
                        bk = B[4 + half]
                        hs = slice(half * 512, (half + 1) * 512)
                        for k in range(8):
                            S.op("pe", lambda e, k=k, blk=blk, hs=hs, bk=bk: e.matmul(
                                bk[:, :], uT_[:, k, blk * 128:(blk + 1) * 128], FWB[:, k, hs],
                                start=(k == 0), stop=(k == 7)), r=[uT_, WLf.rb(0)], w=[bk], inc=(k == 7))
                        tt = tts[half]
                        S.op("dve", lambda e, hs=hs, bk=bk, tt=tt: e.tensor_tensor(
                            out=tt[:], in0=bk[:, :], in1=FBB[:, hs], op=ALU.add), r=[bk, FBB], w=[tt])
                        S.op("dve", lambda e, hs=hs, tt=tt: e.tensor_tensor(
                            out=tt[:], in0=tt[:], in1=GT1[:, hs], op=ALU.mult), r=[tt, GT1], w=[tt])
                        S.op("dve", lambda e, hs=hs, tt=tt, xt=xt, xo=xo: e.tensor_tensor(
                            out=xo[:, hs], in0=tt[:], in1=xt[:, hs], op=ALU.add), r=[tt, xt], w=[xo])
                    S.dma("sp", XA[g0:g0 + 128, :], xo[:], r=[xo])
            S.barrier()
            phb.close()

    if upto == "k7":
        return finish()
    ffn(1, XA, y_out, True)
    return finish()


def _consts(Ls):
    bf = ml_dtypes.bfloat16
    i = np.arange(128)
    sp, q = i[:, None], i[None, :]
    cb = np.zeros((128, 9, 128), np.float32)
    cb[:, 0] = np.eye(128)
    cb[:, 1] = (sp <= q)
    cb[:, 2] = (sp > q)
    cb[:, 3] = (sp >= q)
    cb[:, 4] = (sp < q)
    cb[:, 5] = np.where(q >= sp, 0.0, -30000.0)
    cb[:, 6] = np.where(q <= sp, 0.0, -30000.0)
    cb[:, 7] = 1.0
    cb[:, 8] = 1.0 / 1024.0
    ang = 2.0 * np.pi * (i[:, None] * i[None, :] % 128) / 128.0
    ccs = np.stack([np.cos(ang), -np.sin(ang)], axis=1)
    out = {"consts_bf": cb.astype(bf), "consts_f32": np.eye(128, dtype=np.float32), "dft_c": ccs.astype(bf)}
    for L in Ls:
        l = np.arange(L, dtype=np.int64)
        prod = (l[:, None] * l[None, :]) % L
        a = 2.0 * np.pi * prod.astype(np.float64) / L
        for nm, M in (("dft_cl_%d" % L, np.cos(a)), ("dft_sl_%d" % L, np.sin(a))):
            M4 = M.reshape(L // 128, 128, L // 256, 256).transpose(2, 1, 0, 3)
            out[nm] = np.ascontiguousarray(M4).astype(bf)
    return out


def _shared_inputs(inp):
    f = lambda a: np.ascontiguousarray(np.asarray(a, dtype=np.float32))
    d = {}
    d["w_mod"] = f(inp["w_mod"])
    d["b_mod"] = f(inp["b_mod"]).reshape(1, -1)
    d["g_rows"] = np.concatenate([f(inp["g_mix"]).reshape(-1), f(inp["g_ffn"]).reshape(-1)]).reshape(1, -1)
    d["w_in"] = f(inp["w_in"])[0]
    d["ssd_conv_wT"] = np.ascontiguousarray(f(inp["ssd_conv_w"])[0].reshape(5, 16, 128).transpose(2, 1, 0))
    d["ssd_conv_bT"] = np.ascontiguousarray(f(inp["ssd_conv_b"])[0].reshape(16, 128).T)
    d["dt_bias"] = f(inp["ssd_dt_bias"])[0].reshape(1, 32)
    d["a_log"] = f(inp["ssd_a_log"])[0].reshape(1, 32)
    d["d_row"] = np.repeat(f(inp["ssd_d"])[0], 64).reshape(1, D)
    d["ssd_norm"] = f(inp["ssd_norm"])[0].reshape(1, D)
    d["conf_conv_wT"] = np.ascontiguousarray(f(inp["conf_conv_w"])[0].reshape(31, 8, 128).transpose(2, 1, 0))
    d["conf_conv_bT"] = np.ascontiguousarray(f(inp["conf_conv_b"])[0].reshape(8, 128).T)
    d["ln_gT"] = np.ascontiguousarray(f(inp["conf_ln_g"])[0].reshape(8, 128).T)
    d["ln_bT"] = np.ascontiguousarray(f(inp["conf_ln_b"])[0].reshape(8, 128).T)
    d["w_out"] = f(inp["w_out"])[0]
    d["fnet_w"] = f(inp["fnet_w"])[0]
    d["fnet_b"] = f(inp["fnet_b"])[0].reshape(1, D)
    d["ffn_w_up"] = f(inp["ffn_w_up"])
    fw = f(inp["ffn_conv_w"]).reshape(2, 3, 3, 2, 22, 128)
    fw = np.stack([fw[:, dy, dx] for (dy, dx) in ORDER9], axis=1)
    d["ffn_conv_wT"] = np.ascontiguousarray(fw.transpose(0, 4, 3, 2, 1).reshape(2, 128, 22, 18))
    d["ffn_w_down"] = f(inp["ffn_w_down"])
    d["norm_f"] = f(inp["norm_f"]).reshape(1, D)
    return d


def run(inp, seqs, core_items, trace=False, upto=None, debug=False):
    nc = build(seqs, upto=upto, debug=debug)
    shared = _shared_inputs(inp)
    shared.update(_consts(sorted(set(L for _, L in seqs))))
    xs = np.asarray(inp["x_sample"], np.float32)
    xp = np.asarray(inp["x_prompt"], np.float32)
    cc = np.asarray(inp["c"], np.float32)
    cctx = np.asarray(inp["c_ctx"], np.float32)
    st = np.asarray(inp["state_ssd"], np.float32)
    Ls_ = [L for k, L in seqs if k == "s"][0]
    Lp_ = [L for k, L in seqs if k == "p"][0]
    in_maps = []
    for (si, pis) in core_items:
        m = dict(shared)
        m["x"] = np.ascontiguousarray(np.concatenate([xs[si, :Ls_]] + [xp[p, :Lp_] for p in pis], axis=0))
        cond = np.stack([cc[si], cctx], axis=1)
        m["condT"] = np.ascontiguousarray(cond.reshape(8, 128, 2).transpose(1, 0, 2))
        m["state0"] = np.ascontiguousarray(st[si, 0].reshape(2, D, 128))
        in_maps.append(m)
    res = run_bass_kernel_spmd(nc, in_maps, core_ids=list(range(len(core_items))), trace=trace)
    return res


def kernel(**inp):
    nP = 4
    seqs = [("s", 4096)] + [("p", 256)] * nP
    core_items = [(i, list(range(4 * i, 4 * i + 4))) for i in range(8)]
    res = run(inp, seqs, core_items)
    y_prompt = np.zeros((32, 256, D), np.float32)
    y_sample = np.zeros((8, 4096, D), np.float32)
    new_state = np.zeros((32, 1, 2, 16, 64, 128), np.float32)
    for i, r in enumerate(res.results):
        y = np.asarray(r["y"])
        y_sample[i] = y[:4096]
        y_prompt[4 * i:4 * i + 4] = y[4096:].reshape(4, 256, D)
        new_state[4 * i:4 * i + 4, 0] = np.asarray(r["st"]).reshape(4, 2, 16, 64, 128)
    return (y_prompt, y_sample, new_state)
```

```python
import contextlib
import numpy as np
import ml_dtypes
import concourse.bass as bass
import concourse.mybir as mybir
from concourse.bass_utils import run_bass_kernel_spmd

F32 = mybir.dt.float32
BF16 = mybir.dt.bfloat16
AF = mybir.ActivationFunctionType
ALU = mybir.AluOpType

D = 1024
DIN = 5152
DFF = 2816
EPS = 1e-6
PAD = 128
ORDER9 = [(0, 1), (1, 1), (2, 1), (0, 0), (0, 2), (1, 0), (1, 2), (2, 0), (2, 2)]
import os
LATE = os.environ.get("LATE", "1") == "1"
SAME_ENGINE_SYNC = set(os.environ.get("SAME_SYNC", "dve").split(","))


class Buf:
    __slots__ = ("w", "r")

    def __init__(self):
        self.w = None
        self.r = {}


class T:
    def __init__(self, t):
        self.t = t
        self.b = Buf()

    def __getitem__(self, k):
        return self.t[k]


class Sched:
    def __init__(self, nc):
        self.nc = nc
        self.E = {"pe": nc.tensor, "act": nc.scalar, "dve": nc.vector, "pool": nc.gpsimd, "sp": nc.sync}
        self.sem = {k: nc.alloc_semaphore(name="es_" + k) for k in self.E}
        self.cnt = {k: 0 for k in self.E}
        self.seen = {k: {} for k in self.E}
        self.rings = {}
        for q, n in (("sp", 24), ("act", 8), ("pool", 4)):
            self.rings[q] = dict(sems=[nc.alloc_semaphore(name="ds_%s%d" % (q, i)) for i in range(n)],
                                 uses=[0] * n, n=0)

    def semof(self, k):
        if isinstance(k, str):
            return self.sem[k]
        return self.rings[k[1]]["sems"][k[2]]

    def _waits(self, e, r, w, extra=()):
        need = {}

        def add(ev):
            if ev is None:
                return
            k, v = ev
            if k == e and e not in SAME_ENGINE_SYNC:
                return
            if self.seen[e].get(k, 0) >= v:
                return
            if need.get(k, 0) < v:
                need[k] = v

        for b in r:
            add(b.w)
        for b in w:
            add(b.w)
            for k, v in b.r.items():
                add((k, v))
        for ev in extra:
            add(ev)
        eng = self.E[e]
        for k, v in need.items():
            eng.wait_ge(self.semof(k), v)
            self.seen[e][k] = v

    @staticmethod
    def _bufs(xs):
        return [x.b if isinstance(x, T) else x for x in xs]

    def _commit(self, ev, r, w):
        for b in w:
            b.w = ev
            b.r = {}
        for b in r:
            if b.r.get(ev[0], 0) < ev[1]:
                b.r[ev[0]] = ev[1]

    def op(self, e, fn, r=(), w=(), inc=True):
        r = self._bufs(r)
        w = self._bufs(w)
        self._waits(e, r, w)
        ins = fn(self.E[e])
        if inc:
            self.cnt[e] += 1
            ins.then_inc(self.sem[e], 1)
            ev = (e, self.cnt[e])
        else:
            ev = (e, self.cnt[e] + 1)
        self._commit(ev, r, w)

    def dma(self, q, out, in_, r=(), w=()):
        r = self._bufs(r)
        w = self._bufs(w)
        ring = self.rings[q]
        slot = ring["n"] % len(ring["sems"])
        ring["n"] += 1
        key = ("d", q, slot)
        extra = []
        if ring["uses"][slot] > 0:
            extra.append((key, 16 * ring["uses"][slot]))
        self._waits(q, r, w, extra)
        ins = self.E[q].dma_start(out=out, in_=in_)
        ring["uses"][slot] += 1
        ins.then_inc(ring["sems"][slot], 16)
        ev = (key, 16 * ring["uses"][slot])
        self._commit(ev, r, w)

    def all_events(self):
        evs = [(k, v) for k, v in self.cnt.items() if v > 0]
        for q, ring in self.rings.items():
            for i, u in enumerate(ring["uses"]):
                if u > 0:
                    evs.append((("d", q, i), 16 * u))
        return evs

    def barrier(self, engines=None):
        evs = self.all_events()
        for e in (engines or self.E):
            eng = self.E[e]
            for k, v in evs:
                if self.seen[e].get(k, 0) >= v:
                    continue
                eng.wait_ge(self.semof(k), v)
                self.seen[e][k] = v


def build(seqs, upto=None, debug=False):
    nc = bass.Bass("TRN2", target_bir_lowering=False)
    S = Sched(nc)
    nseq = len(seqs)
    NP = sum(1 for k, _ in seqs if k == "p")
    tok0, ptok0 = [], []
    t = 0
    pt = PAD
    for k, L in seqs:
        tok0.append(t)
        ptok0.append(pt)
        t += L
        pt += L + PAD
    NTOK, NPTOK = t, pt
    NCH = NTOK // 128
    Ls = sorted(set(L for _, L in seqs))

    def din(name, shape, dt=F32):
        return nc.dram_tensor(name, list(shape), dt, kind="ExternalInput").ap()

    def dscr(name, shape, dt=F32):
        return nc.dram_tensor(name, list(shape), dt, kind="ExternalOutput" if debug else "Internal").ap()

    def finish():
        S.barrier(["sp"])
        ES.close()
        return nc

    x_in = din("x", [NTOK, D])
    condT = din("condT", [128, 8, 2])
    state0 = din("state0", [2, D, 128])
    w_mod = din("w_mod", [2, D, 6 * D])
    b_mod = din("b_mod", [1, 12 * D])
    g_rows = din("g_rows", [1, 4 * D])
    w_in = din("w_in", [D, DIN])
    cwT = din("ssd_conv_wT", [128, 16, 5])
    cbT = din("ssd_conv_bT", [128, 16])
    dtb = din("dt_bias", [1, 32])
    alog = din("a_log", [1, 32])
    drow = din("d_row", [1, D])
    ngrow = din("ssd_norm", [1, D])
    ccwT = din("conf_conv_wT", [128, 8, 31])
    ccbT = din("conf_conv_bT", [128, 8])
    lngT = din("ln_gT", [128, 8])
    lnbT = din("ln_bT", [128, 8])
    w_out = din("w_out", [2 * D, D])
    fnet_w = din("fnet_w", [D, D])
    fnet_b = din("fnet_b", [1, D])
    w_up = din("ffn_w_up", [2, D, 2 * DFF])
    fcwT = din("ffn_conv_wT", [2, 128, 22, 18])
    w_down = din("ffn_w_down", [2, DFF, D])
    nf_row = din("norm_f", [1, D])
    cbf = din("consts_bf", [128, 9, 128], BF16)
    cf32 = din("consts_f32", [128, 128])
    ccs = din("dft_c", [128, 2, 128], BF16)
    dftL = {L: (din("dft_cl_%d" % L, [L // 256, 128, L // 128, 256], BF16),
                din("dft_sl_%d" % L, [L // 256, 128, L // 128, 256], BF16)) for L in Ls}

    y_out = nc.dram_tensor("y", [NTOK, D], F32, kind="ExternalOutput").ap()
    st_out = nc.dram_tensor("st", [max(NP, 1), 2, D, 128], F32, kind="ExternalOutput").ap()

    MODROW = dscr("modrow", [2, 12 * D])
    ZS = dscr("zs", [NTOK, D])
    YP = dscr("yp", [NTOK, D])
    XA = dscr("xa", [NTOK, D])
    XB = dscr("xb", [NTOK, D])
    XBCR = dscr("xbcr", [2 * D, NPTOK], BF16)
    GLU = dscr("glu", [D, NPTOK], BF16)
    UR = dscr("ur", [2 * DFF, NPTOK], BF16)
    CTS = dscr("cts", [NCH, 128, 2560], BF16)

    ES = contextlib.ExitStack()

    def sb(shape, dt=F32, stack=None):
        return T((stack or ES).enter_context(nc.sbuf_tensor(nc.make_name("sb", True), list(shape), dt)))

    def ps(shape, dt=F32):
        return T(ES.enter_context(nc.psum_tensor(nc.make_name("ps", True), list(shape), dt)))

    B = [ps([128, 512]) for _ in range(7)]
    TB = ps([128, 1024], BF16)

    CB_ = sb([128, 9, 128], BF16)
    CF = sb([128, 128])
    CCS = sb([128, 2, 128], BF16)
    S.dma("sp", CB_[:], cbf, w=[CB_])
    S.dma("sp", CF[:], cf32, w=[CF])
    S.dma("sp", CCS[:], ccs, w=[CCS])
    IDB = CB_[:, 0, :]
    TRI = {0: (CB_[:, 1, :], CB_[:, 2, :]), 1: (CB_[:, 3, :], CB_[:, 4, :])}
    NEGM = {0: CB_[:, 5, :], 1: CB_[:, 6, :]}
    ONESB = CB_[:, 7, :]
    ONESM = CB_[:, 8, :]
    IDF = CF[:, :]

    DT = sb([128, NCH, 32])
    LA = sb([128, NCH, 32])
    LAH = sb([128, NCH, 32], BF16)
    LAL = sb([128, NCH, 32], BF16)
    ABC = sb([128, 32])
    EPSC = sb([128, 2])
    S.op("pool", lambda e: e.memset(EPSC[:, 0:1], EPS), w=[EPSC])
    S.op("pool", lambda e: e.memset(EPSC[:, 1:2], 1.0), w=[EPSC])
    DTBB = sb([128, 32])
    WCH = 1408
    WST = [sb([128, WCH]) for _ in range(2)]
    wst_n = [0]

    def bc(ap_row, n=128):
        return ap_row.partition_broadcast(n)

    S.dma("sp", ABC[:], bc(alog), w=[ABC])
    S.dma("sp", DTBB[:], bc(dtb), w=[DTBB])
    S.op("act", lambda e: e.activation(out=ABC[:], in_=ABC[:], func=AF.Exp), r=[ABC], w=[ABC])
    S.op("dve", lambda e: e.tensor_scalar(out=ABC[:], in0=ABC[:], scalar1=-1.0, scalar2=None, op0=ALU.mult),
         r=[ABC], w=[ABC])

    cvt_rr = [0]

    class WLoad:
        def __init__(self, dst, src2d, K, N, bounds=None):
            self.bounds = bounds or [0, N]
            self.bufs = [Buf() for _ in range(len(self.bounds) - 1)]
            self.jobs = []
            self.pos = 0
            for b_ in range(len(self.bounds) - 1):
                for k in range(K):
                    c0 = self.bounds[b_]
                    while c0 < self.bounds[b_ + 1]:
                        c1 = min(self.bounds[b_ + 1], c0 + WCH)
                        self.jobs.append((b_, self._mk(dst, src2d, k, c0, c1, self.bufs[b_])))
                        c0 = c1

        def _mk(self, dst, src2d, k, c0, c1, buf):
            def job():
                st = WST[wst_n[0] % 2]
                wst_n[0] += 1
                S.dma("sp", st[:, 0:c1 - c0], src2d[k * 128:(k + 1) * 128, c0:c1], w=[st])
                eng = ("act", "dve")[cvt_rr[0] % 2]
                cvt_rr[0] += 1
                if eng == "act":
                    S.op("act", lambda e: e.activation(out=dst[:, k, c0:c1], in_=st[:, 0:c1 - c0], func=AF.Copy),
                         r=[st], w=[buf])
                else:
                    S.op("dve", lambda e: e.tensor_copy(out=dst[:, k, c0:c1], in_=st[:, 0:c1 - c0]), r=[st], w=[buf])
            return job

        def pump(self, n=1):
            for _ in range(n):
                if self.pos < len(self.jobs):
                    self.jobs[self.pos][1]()
                    self.pos += 1

        def need(self, b_):
            while self.pos < len(self.jobs) and self.jobs[self.pos][0] <= b_:
                self.pump(1)

        def all(self):
            self.pump(len(self.jobs))

        def rb(self, col):
            for i in range(len(self.bufs)):
                if self.bounds[i] <= col < self.bounds[i + 1]:
                    return self.bufs[i]
            raise ValueError(col)

    with contextlib.ExitStack() as ph:
        Z = sb([128, 16, PAD], F32, ph)
        Zb = sb([128, 44, PAD], BF16, ph)
        S.op("pool", lambda e: e.memset(Z[:], 0.0), w=[Z])
        S.op("pool", lambda e: e.memset(Zb[:], 0.0), w=[Zb])
        for scr, C, zt_ in ((XBCR, 16, Zb), (GLU, 8, Zb), (UR, 44, Zb)):
            v = scr.rearrange("(j p) t -> p j t", p=128)
            for s in range(nseq + 1):
                p0 = (ptok0[s] - PAD) if s < nseq else (NPTOK - PAD)
                S.dma("sp", v[:, :, p0:p0 + PAD], zt_[:, 0:C, :], r=[zt_])

        scT = sb([128, 8, 2], F32, ph)
        S.dma("sp", scT[:], condT, w=[scT])
        S.op("act", lambda e: e.activation(out=scT[:], in_=scT[:], func=AF.Silu), r=[scT], w=[scT])
        grow = sb([2, 4 * D], F32, ph)
        S.dma("sp", grow[:], bc(g_rows, 2), w=[grow])
        brow = sb([2, 6 * D], F32, ph)
        modrow = sb([2, 6 * D], F32, ph)
        wm = [sb([128, 8, 512], F32, ph) for _ in range(5)]
        wn_ = 0
        for i in range(2):
            S.dma("sp", brow[:], bc(b_mod[:, i * 6 * D:(i + 1) * 6 * D], 2), w=[brow])
            for m in range(6):
                for half in range(2):
                    slot = wm[wn_ % 5]
                    wn_ += 1
                    o = m * D + half * 512
                    S.dma("sp", slot[:], w_mod[i][:, o:o + 512].rearrange("(k p) n -> p k n", p=128), w=[slot])
                    bk = B[half]
                    for k in range(8):
                        S.op("pe", lambda e, k=k, slot=slot, bk=bk: e.matmul(
                            bk[0:2, :], scT[:, k, :], slot[:, k, :],
                            start=(k == 0), stop=(k == 7)), r=[scT, slot], w=[bk], inc=(k == 7))
                    S.op("dve", lambda e, o=o, bk=bk: e.tensor_tensor(
                        out=modrow[:, o:o + 512], in0=bk[0:2, :], in1=brow[:, o:o + 512], op=ALU.add),
                        r=[bk, brow], w=[modrow])
            for (m, gi) in ((1, i), (4, 2 + i)):
                o = m * D
                S.op("dve", lambda e, o=o, gi=gi: e.scalar_tensor_tensor(
                    out=modrow[:, o:o + D], in0=modrow[:, o:o + D], scalar=1.0, in1=grow[:, gi * D:(gi + 1) * D],
                    op0=ALU.add, op1=ALU.mult), r=[modrow, grow], w=[modrow])
            S.dma("sp", MODROW[:, i * 6 * D:(i + 1) * 6 * D], modrow[:], r=[modrow])
        S.barrier()
    if upto == "k1":
        return finish()

    def modrow_ap(cond, layer, m):
        o = (layer * 6 + m) * D
        return MODROW[cond:cond + 1, o:o + D]

    cond_of = [0 if k == "s" else 1 for k, _ in seqs]

    class CondTiles:
        def __init__(self, ph, specs):
            self.t = [sb([128, D], F32, ph) for _ in specs]
            self.specs = specs
            self.cur = None

        def set(self, c):
            if c == self.cur:
                return
            self.cur = c
            for t_, (layer, m) in zip(self.t, self.specs):
                S.dma("sp", t_[:], bc(modrow_ap(c, layer, m)), w=[t_])

    PTMP = [sb([128, 2]) for _ in range(2)]
    ptn = [0]

    def fma(eng, accT, oview, i_ap, wcol, rT, first, bias=None):
        o_ap = oview(accT)
        if first:
            if bias is not None:
                S.op(eng, lambda e: e.tensor_scalar(out=o_ap, in0=i_ap, scalar1=wcol, scalar2=bias,
                                                    op0=ALU.mult, op1=ALU.add), r=rT, w=[accT])
            else:
                S.op(eng, lambda e: e.tensor_scalar(out=o_ap, in0=i_ap, scalar1=wcol, scalar2=None, op0=ALU.mult),
                     r=rT, w=[accT])
        elif eng == "dve":
            S.op(eng, lambda e: e.scalar_tensor_tensor(out=o_ap, in0=i_ap, scalar=wcol, in1=o_ap,
                                                       op0=ALU.mult, op1=ALU.add), r=rT + [accT], w=[accT])
        else:
            tmp = PTMP[ptn[0] % 2]
            ptn[0] += 1
            t_ap = oview(tmp)
            S.op(eng, lambda e: e.tensor_scalar(out=t_ap, in0=i_ap, scalar1=wcol, scalar2=None, op0=ALU.mult),
                 r=rT, w=[tmp])
            S.op(eng, lambda e: e.tensor_tensor(out=o_ap, in0=o_ap, in1=t_ap, op=ALU.add), r=[accT, tmp], w=[accT])

    def hT_jobs(src, s, t0, TT, Gc, SHc, xts, nm_tiles, hmb, hT, xcount):
        As, Bs = [], []
        nblk = TT // 128
        for blk in range(nblk):
            hb_ = hmb[blk % len(hmb)] if isinstance(hmb, list) else hmb

            def jobA(blk=blk, hb_=hb_):
                xt = xts[xcount[0] % len(xts)]
                xcount[0] += 1
                g0 = tok0[s] + t0 + blk * 128
                S.dma("sp", xt[:], src[g0:g0 + 128, :], w=[xt])
                S.op("act", lambda e: e.activation(out=nm_tiles[0][:], in_=xt[:], func=AF.Square,
                                                   accum_out=nm_tiles[1][:, 0:1]), r=[xt], w=[nm_tiles[0], nm_tiles[1]])
                ss = nm_tiles[1]
                S.op("act", lambda e: e.activation(out=ss[:, 1:2], in_=ss[:, 0:1], func=AF.Sqrt, bias=EPSC[:, 0:1], scale=1.0 / D), r=[ss, EPSC], w=[ss])
                S.op("dve", lambda e: e.reciprocal(out=ss[:, 2:3], in_=ss[:, 1:2]), r=[ss], w=[ss])
                h1 = nm_tiles[2]
                S.op("dve", lambda e: e.scalar_tensor_tensor(out=h1[:], in0=xt[:], scalar=ss[:, 2:3], in1=Gc[:],
                                                             op0=ALU.mult, op1=ALU.mult), r=[xt, ss, Gc], w=[h1])
                S.op("dve", lambda e: e.tensor_tensor(out=hb_[:], in0=h1[:], in1=SHc[:], op=ALU.add),
                     r=[h1, SHc], w=[hb_])

            def jobB(blk=blk, hb_=hb_):
                if hT is None:
                    return
                for k in range(8):
                    S.op("pe", lambda e, k=k: e.transpose(out=TB[:, k * 128:(k + 1) * 128],
                                                          in_=hb_[:, k * 128:(k + 1) * 128], identity=IDB),
                         r=[hb_, CB_], w=[TB], inc=(k == 7))
                S.op("act", lambda e: e.activation(
                    out=hT[:, :, blk * 128:(blk + 1) * 128], in_=TB[:, :].rearrange("p (k t) -> p k t", k=8),
                    func=AF.Copy), r=[TB], w=[hT])
            As.append(jobA)
            Bs.append(jobB)
        if isinstance(hmb, list) and len(hmb) > 1:
            out = [As[0]]
            for i in range(1, nblk):
                out.append(As[i])
                out.append(Bs[i - 1])
            out.append(Bs[nblk - 1])
            return out
        out = []
        for i in range(nblk):
            out += [As[i], Bs[i]]
        return out

    def make_hT(ph, src, s, t0, TT, Gc, SHc, xts, nm_tiles, hmb, hT, xcount):
        for j_ in hT_jobs(src, s, t0, TT, Gc, SHc, xts, nm_tiles, hmb, hT, xcount):
            j_()

    with contextlib.ExitStack() as ph:
        WIN = sb([128, 8, DIN], BF16, ph)
        WLw = WLoad(WIN, w_in, 8, DIN, [0, 1024, 3104, 4128, 5152])
        WLw.need(0)
        CT2 = CondTiles(ph, [(0, 1), (0, 0)])
        xts = [sb([128, D], F32, ph) for _ in range(2)]
        nm = (sb([128, D], BF16, ph), sb([128, 4], F32, ph), sb([128, D], F32, ph))
        hmb = [sb([128, D], BF16, ph) for _ in range(2)]
        hTs = [sb([128, 8, 512], BF16, ph) for _ in range(2)]
        zst = [sb([128, D], F32, ph) for _ in range(2)]
        xst = [sb([128, 8, 512], BF16, ph) for _ in range(2)]
        ust = [sb([128, 4, 512], BF16, ph) for _ in range(2)]
        sg = [sb([128, 512], F32, ph) for _ in range(2)]
        dtt = [sb([128, 32], F32, ph) for _ in range(4)]
        xc = [0]
        tn = 0
        tiles2 = [(s, kind, L, min(512, L), t0) for s, (kind, L) in enumerate(seqs) for t0 in range(0, L, min(512, L))]
        pend = []

        def pump(n=1):
            for _ in range(n):
                if pend:
                    pend.pop(0)()

        def queue_tile(ti):
            s, kind, L, TT, t0 = tiles2[ti]
            pend.append(lambda: CT2.set(cond_of[s]))
            pend.extend(hT_jobs(x_in, s, t0, TT, CT2.t[0], CT2.t[1], xts, nm, hmb, hTs[ti % 2], xc))

        queue_tile(0)
        for ti, (s, kind, L, TT, t0) in enumerate(tiles2):
            nb = TT // 128
            if True:
                pump(len(pend))
                hT = hTs[ti % 2]
                if ti + 1 < len(tiles2):
                    queue_tile(ti + 1)
                for blk in range(nb):
                    zt = zst[blk % 2]
                    for half in range(2):
                        bk = B[half]
                        for k in range(8):
                            S.op("pe", lambda e, k=k, blk=blk, half=half, bk=bk: e.matmul(
                                bk[:, :], hT[:, k, blk * 128:(blk + 1) * 128], WIN[:, k, half * 512:(half + 1) * 512],
                                start=(k == 0), stop=(k == 7)), r=[hT, WLw.rb(0)], w=[bk], inc=(k == 7))
                        S.op("act", lambda e, half=half, bk=bk, zt=zt: e.activation(
                            out=zt[:, half * 512:(half + 1) * 512], in_=bk[:, :], func=AF.Silu), r=[bk], w=[zt])
                        WLw.pump(3)
                    g0 = tok0[s] + t0 + blk * 128
                    S.dma("sp", ZS[g0:g0 + 128, :], zt[:], r=[zt])
                WLw.need(1)
                for blk in range(nb):
                    ch = (tok0[s] + t0) // 128 + blk
                    bk = B[6]
                    for k in range(8):
                        S.op("pe", lambda e, k=k, blk=blk: e.matmul(
                            bk[:, 0:32], hT[:, k, blk * 128:(blk + 1) * 128], WIN[:, k, 3072:3104],
                            start=(k == 0), stop=(k == 7)), r=[hT, WLw.rb(3072)], w=[bk], inc=(k == 7))
                    xx, mm_, aa, ee = dtt
                    S.op("dve", lambda e: e.tensor_tensor(out=xx[:], in0=bk[:, 0:32], in1=DTBB[:], op=ALU.add),
                         r=[bk, DTBB], w=[xx])
                    S.op("dve", lambda e: e.tensor_scalar(out=mm_[:], in0=xx[:], scalar1=0.0, scalar2=None, op0=ALU.max),
                         r=[xx], w=[mm_])
                    S.op("act", lambda e: e.activation(out=aa[:], in_=xx[:], func=AF.Abs), r=[xx], w=[aa])
                    S.op("act", lambda e: e.activation(out=ee[:], in_=aa[:], func=AF.Exp, scale=-1.0), r=[aa], w=[ee])
                    S.op("act", lambda e: e.activation(out=ee[:], in_=ee[:], func=AF.Ln, bias=EPSC[:, 1:2], scale=1.0), r=[ee, EPSC], w=[ee])
                    S.op("dve", lambda e, ch=ch: e.tensor_tensor(out=DT[:, ch, :], in0=mm_[:], in1=ee[:], op=ALU.add),
                         r=[mm_, ee], w=[DT])
                    S.op("dve", lambda e, ch=ch: e.tensor_tensor(out=LA[:, ch, :], in0=DT[:, ch, :], in1=ABC[:],
                                                                 op=ALU.mult), r=[DT, ABC], w=[LA])
                    S.op("dve", lambda e, ch=ch: e.tensor_copy(out=LAH[:, ch, :], in_=LA[:, ch, :]), r=[LA], w=[LAH])
                    S.op("dve", lambda e, ch=ch: e.tensor_tensor(out=LAL[:, ch, :], in0=LA[:, ch, :], in1=LAH[:, ch, :],
                                                                 op=ALU.subtract), r=[LA, LAH], w=[LAL])
                p0 = ptok0[s] + t0
                for j in range(16):
                    xs_ = xst[j // 8]
                    bk = B[j % 2]
                    c0 = D + j * 128
                    for k in range(8):
                        S.op("pe", lambda e, k=k, c0=c0, bk=bk: e.matmul(
                            bk[:, 0:TT], WIN[:, k, c0:c0 + 128], hT[:, k, 0:TT],
                            start=(k == 0), stop=(k == 7)), r=[hT, WLw.rb(c0)], w=[bk], inc=(k == 7))
                    WLw.pump(1)
                    if j % 2 == 0:
                        S.op("act", lambda e, j=j, bk=bk, xs_=xs_: e.activation(out=xs_[:, j % 8, 0:TT], in_=bk[:, 0:TT], func=AF.Copy),
                             r=[bk], w=[xs_])
                    else:
                        S.op("dve", lambda e, j=j, bk=bk, xs_=xs_: e.tensor_copy(out=xs_[:, j % 8, 0:TT], in_=bk[:, 0:TT]),
                             r=[bk], w=[xs_])
                    if j % 4 == 1:
                        pump(2 if j == 1 else 1)
                    if j % 8 == 7:
                        jb = j - 7
                        S.dma("sp", XBCR.rearrange("(j p) t -> p j t", p=128)[:, jb:jb + 8, p0:p0 + TT], xs_[:, :, 0:TT], r=[xs_])
                for j in range(8):
                    us_ = ust[j // 4]
                    ba, bb = B[2 + 2 * (j % 2)], B[3 + 2 * (j % 2)]
                    ca = 3104 + j * 128
                    cb_ = 3104 + D + j * 128
                    WLw.need(3)
                    pump(1)
                    for (bk, c0) in ((ba, ca), (bb, cb_)):
                        for k in range(8):
                            S.op("pe", lambda e, k=k, c0=c0, bk=bk: e.matmul(
                                bk[:, 0:TT], WIN[:, k, c0:c0 + 128], hT[:, k, 0:TT],
                                start=(k == 0), stop=(k == 7)), r=[hT, WLw.rb(c0)], w=[bk], inc=(k == 7))
                    sgt = sg[j % 2]
                    S.op("act", lambda e, bb=bb, sgt=sgt: e.activation(out=sgt[:, 0:TT], in_=bb[:, 0:TT], func=AF.Sigmoid),
                         r=[bb], w=[sgt])
                    S.op("dve", lambda e, j=j, ba=ba, sgt=sgt, us_=us_: e.tensor_tensor(
                        out=us_[:, j % 4, 0:TT], in0=ba[:, 0:TT], in1=sgt[:, 0:TT], op=ALU.mult), r=[ba, sgt], w=[us_])
                    if j % 4 == 3:
                        jb = j - 3
                        S.dma("sp", GLU.rearrange("(j p) t -> p j t", p=128)[:, jb:jb + 4, p0:p0 + TT], us_[:, :, 0:TT], r=[us_])
        S.barrier()

    if upto == "k2":
        return finish()

    def ssd_chunk(d, ch, xs_tok, bm_tok, bmT, cmT, rT, H, Hb, W, yfin, hook=None, xsD=None, xsDT=None, ypacc=None):
        E3, nacs, wd, xd, xdd, Ebuf, Mbuf, t1s = W["E3"], W["nacs"], W["wd"], W["xd"], W["xdd"], W["E"], W["M"], W["t1"]
        c0 = d * 16
        tiny = B[5]
        TO = 256
        for (col, lhs) in ((0, TRI[d][0]), (16, TRI[d][1]), (32, ONESB)):
            S.op("pe", lambda e, col=col, lhs=lhs: e.matmul(tiny[:, TO + col:TO + col + 16], lhs, LAH[:, ch, c0:c0 + 16],
                                                            start=True, stop=False), r=[CB_, LAH], w=[tiny], inc=False)
            S.op("pe", lambda e, col=col, lhs=lhs: e.matmul(tiny[:, TO + col:TO + col + 16], lhs, LAL[:, ch, c0:c0 + 16],
                                                            start=False, stop=True), r=[CB_, LAL], w=[tiny], inc=True)
        S.op("act", lambda e: e.activation(out=E3[:], in_=tiny[:, TO:TO + 48], func=AF.Exp), r=[tiny], w=[E3])
        S.op("dve", lambda e: e.tensor_scalar(out=nacs[:], in0=tiny[:, TO:TO + 16], scalar1=-1.0, scalar2=None, op0=ALU.mult),
             r=[tiny], w=[nacs])
        S.op("dve", lambda e: e.tensor_tensor(out=wd[:], in0=DT[:, ch, c0:c0 + 16], in1=E3[:, 16:32], op=ALU.mult),
             r=[DT, E3], w=[wd])
        xs3 = xs_tok.rearrange("p (h q) -> p h q", h=16)
        S.op("dve", lambda e: e.tensor_tensor(out=xd[:, :].rearrange("p (h q) -> p h q", h=16), in0=xs3,
                                              in1=DT[:, ch, c0:c0 + 16].unsqueeze(2).to_broadcast([128, 16, 64]),
                                              op=ALU.mult), r=rT + [DT], w=[xd])
        def late_dve(which):
            if which == 0:
                S.op("dve", lambda e: e.tensor_tensor(out=xdd[:, :].rearrange("p (h q) -> p h q", h=16), in0=xs3,
                                                      in1=wd[:, :].unsqueeze(2).to_broadcast([128, 16, 64]),
                                                      op=ALU.mult), r=rT + [wd], w=[xdd])
            else:
                S.op("dve", lambda e: e.tensor_tensor(
                    out=H[:, :].rearrange("p (h q) -> p h q", h=16), in0=H[:, :].rearrange("p (h q) -> p h q", h=16),
                    in1=E3[:, 32:48].unsqueeze(2).to_broadcast([128, 16, 64]), op=ALU.mult), r=[H, E3], w=[H])
        if not LATE:
            late_dve(1)
            late_dve(0)
        cbk = B[0]
        for g in range(4):
            S.op("pe", lambda e, g=g: e.matmul(cbk[:, g * 128:(g + 1) * 128], bmT(g), cmT(g), start=True, stop=True),
                 r=rT, w=[cbk], inc=(g == 3))
        if hook is not None:
            hook(2)
        def stageA(g):
            seg = B[1 + g % 2]
            Eb = Ebuf[g % 2]
            Mb = Mbuf[g % 2]
            for hl in range(4):
                h = g * 4 + hl
                o = seg[:, hl * 128:(hl + 1) * 128]
                S.op("pe", lambda e, o=o, h=h: e.matmul(o, LAH[:, ch, c0 + h:c0 + h + 1].to_broadcast([128, 128]),
                                                        TRI[d][0], start=True, stop=False),
                     r=[LAH, CB_], w=[seg], inc=False)
                S.op("pe", lambda e, o=o, h=h: e.matmul(o, LAL[:, ch, c0 + h:c0 + h + 1].to_broadcast([128, 128]),
                                                        TRI[d][0], start=False, stop=False),
                     r=[LAL, CB_], w=[seg], inc=False)
                S.op("pe", lambda e, o=o: e.matmul(o, IDB, NEGM[d], start=False, stop=True),
                     r=[CB_], w=[seg], inc=(hl == 3))
            for hl in range(4):
                h = g * 4 + hl
                S.op("act", lambda e, hl=hl, h=h: e.activation(
                    out=Eb[:, hl * 128:(hl + 1) * 128], in_=seg[:, hl * 128:(hl + 1) * 128], func=AF.Exp,
                    bias=nacs[:, h:h + 1], scale=1.0), r=[seg, nacs], w=[Eb])
            S.op("dve", lambda e: e.tensor_tensor(
                out=Mb[:, :].rearrange("p (h q) -> p h q", h=4), in0=Eb[:, :].rearrange("p (h q) -> p h q", h=4),
                in1=cbk[:, g * 128:(g + 1) * 128].unsqueeze(1).to_broadcast([128, 4, 128]), op=ALU.mult),
                r=[Eb, cbk], w=[Mb])

        def stageB(g):
            Yg = B[3 + g % 2]
            Mb = Mbuf[g % 2]
            if xsD is not None:
                S.op("pe", lambda e: e.matmul(Yg[:, 0:256], IDB, xsD[:, g * 256:(g + 1) * 256], start=True, stop=False),
                     r=[CB_, xsDT], w=[Yg], inc=False)
            if ypacc is not None:
                S.op("pe", lambda e: e.matmul(Yg[:, 0:256], IDF, ypacc[:, g * 256:(g + 1) * 256], start=True, stop=False),
                     r=[CF, ypacc], w=[Yg], inc=False)
            for hl in range(4):
                h = g * 4 + hl
                S.op("pe", lambda e, hl=hl, h=h: e.matmul(
                    Yg[:, hl * 64:(hl + 1) * 64], Mb[:, hl * 128:(hl + 1) * 128], xd[:, h * 64:(h + 1) * 64],
                    start=(xsD is None and ypacc is None), stop=True), r=[Mb, xd], w=[Yg], inc=False)
            S.op("pe", lambda e: e.matmul(Yg[:, 256:512], cmT(g), Hb[:, g * 256:(g + 1) * 256],
                                          start=True, stop=True), r=rT + [Hb], w=[Yg], inc=True)
            t1 = t1s[g % 2]
            S.op("dve", lambda e: e.tensor_tensor(
                out=t1[:, :].rearrange("p (h q) -> p h q", h=4), in0=Yg[:, 256:512].rearrange("p (h q) -> p h q", h=4),
                in1=E3[:, g * 4:(g + 1) * 4].unsqueeze(2).to_broadcast([128, 4, 64]), op=ALU.mult),
                r=[Yg, E3], w=[t1])
            yfin(g, Yg, t1)

        stageA(0)
        for g in range(4):
            if g + 1 < 4:
                stageA(g + 1)
            if g < 2 and LATE:
                late_dve(g)
            if hook is not None:
                hook(1)
            stageB(g)
        for g in range(4):
            sgk = B[5]
            S.op("pe", lambda e, g=g, sgk=sgk: e.matmul(sgk[:, 0:256], bm_tok[:, g * 128:(g + 1) * 128],
                                                        xdd[:, g * 256:(g + 1) * 256], start=True, stop=True),
                 r=rT + [xdd], w=[sgk], inc=True)
            Hg = H[:, g * 256:(g + 1) * 256]
            S.op("dve", lambda e, Hg=Hg, sgk=sgk: e.tensor_tensor(out=Hg, in0=Hg, in1=sgk[:, 0:256], op=ALU.add),
                 r=[H, sgk], w=[H])
            S.op("act", lambda e, g=g, Hg=Hg: e.activation(out=Hb[:, g * 256:(g + 1) * 256], in_=Hg, func=AF.Copy),
                 r=[H], w=[Hb])
            if hook is not None:
                hook(1)

    def ssd_work(ph):
        return dict(E3=sb([128, 48], F32, ph), nacs=sb([128, 16], F32, ph), wd=sb([128, 16], F32, ph),
                    xd=sb([128, D], BF16, ph), xdd=sb([128, D], BF16, ph),
                    E=[sb([128, 512], F32, ph) for _ in range(2)], M=[sb([128, 512], BF16, ph) for _ in range(2)],
                    t1=[sb([128, 256], F32, ph) for _ in range(2)])

    def init_state(kind, d, H, Hb, stg):
        if kind == "p":
            S.op("pool", lambda e: e.memset(H[:], 0.0), w=[H])
            S.op("pool", lambda e: e.memset(Hb[:], 0.0), w=[Hb])
            return
        S.dma("sp", stg[:], state0[d].rearrange("(k p) n -> p k n", p=128), w=[stg])
        for half in range(2):
            bk = B[3 + half]
            for kk in range(4):
                k = half * 4 + kk
                S.op("pe", lambda e, k=k, kk=kk, bk=bk: e.matmul(bk[:, kk * 128:(kk + 1) * 128], stg[:, k, :], IDF,
                                                                 start=True, stop=True), r=[stg, CF], w=[bk], inc=(kk == 3))
            S.op("act", lambda e, half=half, bk=bk: e.activation(out=H[:, half * 512:(half + 1) * 512], in_=bk[:, :],
                                                                 func=AF.Copy), r=[bk], w=[H])
        S.op("act", lambda e: e.activation(out=Hb[:], in_=H[:], func=AF.Copy), r=[H], w=[Hb])

    def final_state(pidx, d, H, stg):
        for half in range(2):
            for kk in range(4):
                k = half * 4 + kk
                bk = B[3 + kk % 2]
                S.op("pe", lambda e, k=k, bk=bk: e.matmul(bk[:, 0:128], H[:, k * 128:(k + 1) * 128], IDF,
                                                          start=True, stop=True), r=[H, CF], w=[bk], inc=True)
                S.op("act", lambda e, k=k, bk=bk: e.activation(out=stg[:, k, :], in_=bk[:, 0:128], func=AF.Copy),
                     r=[bk], w=[stg])
        S.dma("sp", st_out[pidx, d].rearrange("(k p) n -> p k n", p=128), stg[:], r=[stg])

    with contextlib.ExitStack() as ph:
        CW = sb([128, 16, 5], F32, ph)
        CBI = sb([128, 16], F32, ph)
        DBC = sb([128, D], F32, ph)
        S.dma("sp", CW[:], cwT, w=[CW])
        S.dma("sp", CBI[:], cbT, w=[CBI])
        S.dma("sp", DBC[:], bc(drow), w=[DBC])
        wins = [sb([128, 16, 516], BF16, ph) for _ in range(2)]
        DG5 = [sb([128, 5, 128], BF16, ph) for _ in range(2)]
        XF = [sb([128, 16, 512], BF16, ph) for _ in range(2)]
        toks = [sb([128, 1536], BF16, ph) for _ in range(2)]
        yps = [sb([128, D], F32, ph) for _ in range(2)]
        xsds = [sb([128, D], BF16, ph) for _ in range(2)]
        Wk = ssd_work(ph)
        H = sb([128, D], F32, ph)
        Hb = sb([128, D], BF16, ph)
        stg = sb([128, 8, 128], F32, ph)
        cn = 0
        pidx = 0
        tiles3 = [(s, kind, L, min(512, L), t0) for s, (kind, L) in enumerate(seqs) for t0 in range(0, L, min(512, L))]
        pend = []
        dgn = [0]

        def pump(n=1):
            for _ in range(n):
                if pend:
                    pend.pop(0)()

        def queue_conv(ti):
            s, kind, L, TT, t0 = tiles3[ti]
            win = wins[ti % 2]
            xf = XF[ti % 2]
            p0 = ptok0[s] + t0

            def ld():
                S.dma("sp", win[:, :, 0:TT + 4], XBCR.rearrange("(j p) t -> p j t", p=128)[:, :, p0 - 2:p0 + TT + 2], w=[win])
            pend.append(ld)
            for j in range(16):
                def job(j=j):
                    dg = DG5[dgn[0] % 2]
                    dgn[0] += 1
                    bk = B[6]
                    S.op("dve", lambda e: e.tensor_tensor(
                        out=dg[:, :, :], in0=IDB.unsqueeze(1).to_broadcast([128, 5, 128]),
                        in1=CW[:, j, :].unsqueeze(2).to_broadcast([128, 5, 128]), op=ALU.mult), r=[CB_, CW], w=[dg])
                    for k in range(5):
                        S.op("pe", lambda e, k=k: e.matmul(bk[:, 0:TT], dg[:, k, :], win[:, j, k:k + TT],
                                                           start=(k == 0), stop=(k == 4)), r=[dg, win], w=[bk], inc=(k == 4))
                    S.op("act", lambda e: e.activation(out=xf[:, j, 0:TT], in_=bk[:, 0:TT], func=AF.Silu,
                                                       bias=CBI[:, j:j + 1], scale=1.0), r=[bk, CBI], w=[xf])
                pend.append(job)

        queue_conv(0)
        pump(len(pend))
        prev_s = -1
        for ti, (s, kind, L, TT, t0) in enumerate(tiles3):
            nb = TT // 128
            if ti + 1 < len(tiles3):
                queue_conv(ti + 1)
                pump(len(pend))
            if s != prev_s:
                init_state(kind, 0, H, Hb, stg)
                prev_s = s
            if True:
                xf = XF[ti % 2]
                for blk in range(nb):
                    ch = (tok0[s] + t0) // 128 + blk
                    tk = toks[cn % 2]
                    yp = yps[cn % 2]
                    cn += 1
                    sl = slice(blk * 128, (blk + 1) * 128)
                    for j in range(8):
                        S.op("pe", lambda e, j=j, sl=sl: e.transpose(out=TB[:, j * 128:(j + 1) * 128], in_=xf[:, j, sl],
                                                                     identity=IDB), r=[xf, CB_], w=[TB], inc=(j == 7))
                    S.op("act", lambda e, tk=tk: e.activation(out=tk[:, 0:1024], in_=TB[:, 0:1024], func=AF.Copy),
                         r=[TB], w=[tk])
                    for j in range(4):
                        S.op("pe", lambda e, j=j, sl=sl: e.transpose(out=TB[:, j * 128:(j + 1) * 128], in_=xf[:, 8 + j, sl],
                                                                     identity=IDB), r=[xf, CB_], w=[TB], inc=(j == 3))
                    S.op("act", lambda e, tk=tk: e.activation(out=tk[:, 1024:1536], in_=TB[:, 0:512], func=AF.Copy),
                         r=[TB], w=[tk])
                    S.dma("sp", CTS[ch][:, 0:1536], tk[:, :], r=[tk])
                    S.dma("sp", CTS[ch][:, 1536:2560].rearrange("p (j t) -> p j t", j=8), xf[:, 8:16, sl], r=[xf])

                    xsd = xsds[cn % 2]
                    S.op("pool", lambda e, tk=tk, xsd=xsd: e.tensor_tensor(out=xsd[:], in0=tk[:, 0:1024], in1=DBC[:],
                                                                           op=ALU.mult), r=[tk, DBC], w=[xsd])

                    def yfin(g, Yg, t1, tk=tk, yp=yp):
                        gs = slice(g * 256, (g + 1) * 256)
                        S.op("dve", lambda e: e.tensor_tensor(out=yp[:, gs], in0=Yg[:, 0:256], in1=t1[:], op=ALU.add),
                             r=[Yg, t1], w=[yp])

                    ssd_chunk(0, ch, tk[:, 0:1024], tk[:, 1024:1536],
                              lambda g, sl=sl: xf[:, 8 + g, sl], lambda g, sl=sl: xf[:, 12 + g, sl],
                              [tk, xf], H, Hb, Wk, yfin, xsD=xsd[:, :], xsDT=xsd)
                    g0 = ch * 128
                    S.dma("sp", YP[g0:g0 + 128, :], yp[:], r=[yp])
            if kind == "p" and t0 + TT >= L:
                final_state(pidx, 0, H, stg)
                pidx += 1
        S.barrier()

    if upto == "k3":
        return finish()

    def ffn(layer, src, dst, final):
        with contextlib.ExitStack() as ph:
            WUP = sb([128, 8, 2 * DFF], BF16, ph)
            WLu = WLoad(WUP, w_up[layer], 8, 2 * DFF, [0, 1408, 2816, 4224, 5632])
            WLu.need(0)
            CT5 = CondTiles(ph, [(layer, 4), (layer, 3)])
            xts = [sb([128, D], F32, ph) for _ in range(2)]
            nm = (sb([128, D], BF16, ph), sb([128, 4], F32, ph), sb([128, D], F32, ph))
            hmb = [sb([128, D], BF16, ph) for _ in range(2)]
            hTs = [sb([128, 8, 512], BF16, ph) for _ in range(2)]
            ust = [sb([128, 4, 512], BF16, ph) for _ in range(2)]
            xc = [0]
            un = 0
            tiles5 = [(s, kind, L, min(512, L), t0) for s, (kind, L) in enumerate(seqs) for t0 in range(0, L, min(512, L))]
            pend = []

            def pump(n=1):
                for _ in range(n):
                    if pend:
                        pend.pop(0)()

            def queue_tile(ti):
                s, kind, L, TT, t0 = tiles5[ti]
                pend.append(lambda: CT5.set(cond_of[s]))
                pend.extend(hT_jobs(src, s, t0, TT, CT5.t[0], CT5.t[1], xts, nm, hmb, hTs[ti % 2], xc))

            queue_tile(0)
            for ti, (s, kind, L, TT, t0) in enumerate(tiles5):
                if True:
                    pump(len(pend))
                    hT = hTs[ti % 2]
                    if ti + 1 < len(tiles5):
                        queue_tile(ti + 1)
                    p0 = ptok0[s] + t0
                    for jg in range(11):
                        if 1 <= jg <= 9:
                            pump(2 if jg == 1 else 1)
                        us_ = ust[un % 2]
                        un += 1
                        WLu.pump(3)
                        for jj in range(4):
                            j = jg * 4 + jj
                            bk = B[j % 4]
                            WLu.need(j // 11)
                            for k in range(8):
                                S.op("pe", lambda e, k=k, j=j, bk=bk: e.matmul(
                                    bk[:, 0:TT], WUP[:, k, j * 128:(j + 1) * 128], hT[:, k, 0:TT],
                                    start=(k == 0), stop=(k == 7)), r=[hT, WLu.rb(j * 128)], w=[bk], inc=(k == 7))
                            if j % 2 == 0:
                                S.op("act", lambda e, jj=jj, bk=bk, us_=us_: e.activation(
                                    out=us_[:, jj, 0:TT], in_=bk[:, 0:TT], func=AF.Copy), r=[bk], w=[us_])
                            else:
                                S.op("dve", lambda e, jj=jj, bk=bk, us_=us_: e.tensor_copy(
                                    out=us_[:, jj, 0:TT], in_=bk[:, 0:TT]), r=[bk], w=[us_])
                        S.dma("sp", UR.rearrange("(j p) t -> p j t", p=128)[:, jg * 4:(jg + 1) * 4, p0:p0 + TT],
                              us_[:, :, 0:TT], r=[us_])
            S.barrier()
        with contextlib.ExitStack() as ph:
            WDN = sb([128, 22, D], BF16, ph)
            WLd = WLoad(WDN, w_down[layer], 22, D)
            FW = sb([128, 22, 18], F32, ph)
            S.dma("sp", FW[:], fcwT[layer], w=[FW])
            CT6 = CondTiles(ph, [(layer, 5)])
            GT2 = CT6.t[0]
            if final:
                NFB = sb([128, D], F32, ph)
                S.dma("sp", NFB[:], bc(nf_row), w=[NFB])
                fin = (sb([128, D], BF16, ph), sb([128, 4], F32, ph))
            wins = [sb([128, 2, 640], BF16, ph) for _ in range(3)]
            DG = [sb([128, 18, 128], BF16, ph) for _ in range(2)]
            sgs = [sb([128, 512], F32, ph) for _ in range(2)]
            aTs = [sb([128, 22, 512], BF16, ph) for _ in range(2)]
            xts = [sb([128, D], F32, ph) for _ in range(2)]
            tts = [sb([128, 512], F32, ph) for _ in range(2)]
            xos = [sb([128, D], F32, ph) for _ in range(2)]
            URv = UR.rearrange("(v j p) t -> p v j t", v=2, p=128)
            xn = [0]
            items = []
            tcount = 0
            for s, (kind, L) in enumerate(seqs):
                TT = min(512, L)
                for t0 in range(0, L, TT):
                    for jj in range(22):
                        items.append((s, kind, L, TT, t0, jj, tcount))
                    tcount += 1

            def taps_of(kind):
                taps = [(dy, dx) for dy in range(3) for dx in range(3)] if kind == "s" else [(1, dx) for dx in range(3)]
                return [tp for tp in taps if tp[1] == 1] + [tp for tp in taps if tp[1] != 1]

            def prep(i):
                s, kind, L, TT, t0, jj, tc = items[i]
                win = wins[i % 3]
                dg = DG[i % 2]
                p0 = ptok0[s] + t0
                if kind == "s":
                    S.dma("sp", win[:, :, 0:TT + 128], URv[:, :, jj, p0 - 64:p0 + TT + 64], w=[win])
                else:
                    S.dma("sp", win[:, :, 0:TT + 2], URv[:, :, jj, p0 - 1:p0 + TT + 1], w=[win])
                S.op("dve", lambda e: e.tensor_tensor(
                    out=dg[:, :, :], in0=IDB.unsqueeze(1).to_broadcast([128, 18, 128]),
                    in1=FW[:, jj, :].unsqueeze(2).to_broadcast([128, 18, 128]), op=ALU.mult),
                    r=[CB_, FW], w=[dg])

            def run_item(i):
                s, kind, L, TT, t0, jj, tc = items[i]
                win = wins[i % 3]
                dg = DG[i % 2]
                sgt = sgs[i % 2]
                bv, bg = B[2 * (i % 2)], B[2 * (i % 2) + 1]
                aT = aTs[tc % 2]
                order = taps_of(kind)
                for (v, bk) in ((0, bv), (1, bg)):
                    for tj, (dy, dx) in enumerate(order):
                        ti = ORDER9.index((dy, dx))
                        if kind == "s":
                            w3 = win[:, v, dy * 64:dy * 64 + TT].rearrange("p (r c) -> p r c", c=64)
                            b3 = bk[:, 0:TT].rearrange("p (r c) -> p r c", c=64)
                            if dx == 1:
                                o_ap, i_ap = bk[:, 0:TT], win[:, v, dy * 64:dy * 64 + TT]
                            elif dx == 0:
                                o_ap, i_ap = b3[:, :, 1:64], w3[:, :, 0:63]
                            else:
                                o_ap, i_ap = b3[:, :, 0:63], w3[:, :, 1:64]
                        else:
                            o_ap, i_ap = bk[:, 0:TT], win[:, v, dx:dx + TT]
                        last = tj == len(order) - 1
                        S.op("pe", lambda e, o_ap=o_ap, i_ap=i_ap, v=v, ti=ti, tj=tj, last=last: e.matmul(
                            o_ap, dg[:, v * 9 + ti, :], i_ap, start=(tj == 0), stop=last),
                            r=[dg, win], w=[bk], inc=last)
                S.op("act", lambda e: e.activation(out=sgt[:, 0:TT], in_=bg[:, 0:TT], func=AF.Silu), r=[bg], w=[sgt])
                S.op("dve", lambda e: e.tensor_tensor(
                    out=aT[:, jj, 0:TT], in0=bv[:, 0:TT], in1=sgt[:, 0:TT], op=ALU.mult), r=[bv, sgt], w=[aT])
                WLd.pump(1)
                if jj != 21:
                    return
                WLd.all()
                CT6.set(cond_of[s])
                for blk in range(TT // 128):
                    g0 = tok0[s] + t0 + blk * 128
                    xt = xts[xn[0] % 2]
                    xo = xos[xn[0] % 2]
                    xn[0] += 1
                    S.dma("sp", xt[:], src[g0:g0 + 128, :], w=[xt])
                    for half in range(2):
                        bk = B[4 + half]
                        hs = slice(half * 512, (half + 1) * 512)
                        for k in range(22):
                            S.op("pe", lambda e, k=k, blk=blk, hs=hs, bk=bk: e.matmul(
                                bk[:, :], aT[:, k, blk * 128:(blk + 1) * 128], WDN[:, k, hs],
                                start=(k == 0), stop=(k == 21)), r=[aT, WLd.rb(0)], w=[bk], inc=(k == 21))
                        tt = tts[half]
                        S.op("dve", lambda e, hs=hs, bk=bk, tt=tt: e.tensor_tensor(
                            out=tt[:], in0=bk[:, :], in1=GT2[:, hs], op=ALU.mult), r=[bk, GT2], w=[tt])
                        S.op("dve", lambda e, hs=hs, tt=tt, xt=xt, xo=xo: e.tensor_tensor(
                            out=xo[:, hs], in0=tt[:], in1=xt[:, hs], op=ALU.add), r=[tt, xt], w=[xo])
                    if final:
                        junk, ss = fin
                        S.op("act", lambda e, xo=xo: e.activation(out=junk[:], in_=xo[:], func=AF.Square,
                                                                  accum_out=ss[:, 0:1]), r=[xo], w=[junk, ss])
                        S.op("act", lambda e: e.activation(out=ss[:, 1:2], in_=ss[:, 0:1], func=AF.Sqrt, bias=EPSC[:, 0:1], scale=1.0 / D), r=[ss, EPSC], w=[ss])
                        S.op("dve", lambda e: e.reciprocal(out=ss[:, 2:3], in_=ss[:, 1:2]), r=[ss], w=[ss])
                        S.op("dve", lambda e, xo=xo: e.scalar_tensor_tensor(
                            out=xo[:], in0=xo[:], scalar=ss[:, 2:3], in1=NFB[:], op0=ALU.mult, op1=ALU.mult),
                            r=[xo, ss, NFB], w=[xo])
                    S.dma("sp", dst[g0:g0 + 128, :], xo[:], r=[xo])

            prep(0)
            for i in range(len(items)):
                if i + 1 < len(items):
                    prep(i + 1)
                run_item(i)
            S.barrier()

    with contextlib.ExitStack() as ph:
        WO = sb([128, 16, D], BF16, ph)
        WLo = WLoad(WO, w_out, 16, D)
        CCW = sb([128, 8, 31], F32, ph)
        CCB = sb([128, 8], F32, ph)
        LNG = sb([128, 8], F32, ph)
        LNB = sb([128, 8], F32, ph)
        NGB = sb([128, D], F32, ph)
        S.dma("sp", CCW[:], ccwT, w=[CCW])
        S.dma("sp", CCB[:], ccbT, w=[CCB])
        S.dma("sp", LNG[:], lngT, w=[LNG])
        S.dma("sp", LNB[:], lnbT, w=[LNB])
        S.dma("sp", NGB[:], bc(ngrow), w=[NGB])
        CT4 = CondTiles(ph, [(0, 2)])
        GT1 = CT4.t[0]
        cts = [sb([128, 2560], BF16, ph) for _ in range(2)]
        yps = [sb([128, D], F32, ph) for _ in range(2)]
        zts = [sb([128, D], F32, ph) for _ in range(1)]
        Wk = ssd_work(ph)
        H = sb([128, D], F32, ph)
        Hb = sb([128, D], BF16, ph)
        stg = sb([128, 8, 128], F32, ph)
        junk = sb([128, D], BF16, ph)
        ss = sb([128, 4], F32, ph)
        ynbs = [sb([128, D], BF16, ph) for _ in range(2)]
        ysT = [sb([128, 8, 512], BF16, ph) for _ in range(1)]
        uT = [sb([128, 8, 512], BF16, ph) for _ in range(1)]
        gwin = sb([128, 8, 542], BF16, ph)
        DG31 = [sb([128, 31, 128], BF16, ph) for _ in range(2)]
        cacc = [sb([128, 512], F32, ph) for _ in range(8)]
        vb = [sb([128, 512], BF16, ph) for _ in range(2)]
        qb = [sb([128, 512], BF16, ph) for _ in range(2)]
        mean_sb = sb([128, 512], F32, ph)
        rstd_sb = sb([128, 512], F32, ph)
        xts = [sb([128, D], F32, ph) for _ in range(1)]
        tts = [sb([128, 512], F32, ph) for _ in range(2)]
        xos = [sb([128, D], F32, ph) for _ in range(1)]
        cn = 0
        xn = 0
        pidx = 0
        for s, (kind, L) in enumerate(seqs):
            CT4.set(cond_of[s])
            TT = min(512, L)
            nb = TT // 128
            init_state(kind, 1, H, Hb, stg)
            for t0 in range(L - TT, -1, -TT):
                yT = ysT[0]
                pend4 = []
                p0 = ptok0[s] + t0

                def ld4(p0=p0, TT=TT):
                    S.dma("sp", gwin[:, :, 0:TT + 30], GLU.rearrange("(j p) t -> p j t", p=128)[:, :, p0 - 15:p0 + TT + 15],
                          w=[gwin])
                pend4.append(ld4)
                for j in range(8):
                    for (k0, k1) in ((0, 8), (8, 16), (16, 24), (24, 31)):
                        def cjob(j=j, TT=TT, k0=k0, k1=k1):
                            dg = DG31[j % 2]
                            cbk_ = B[6]
                            acc = cacc[j]
                            if (k0 == 0 and j == 0) or (k0 == 8 and j < 7):
                                jb_ = j if k0 == 0 else j + 1
                                dgb = DG31[jb_ % 2]
                                S.op("dve", lambda e: e.tensor_tensor(
                                    out=dgb[:, :, :], in0=IDB.unsqueeze(1).to_broadcast([128, 31, 128]),
                                    in1=CCW[:, jb_, :].unsqueeze(2).to_broadcast([128, 31, 128]), op=ALU.mult),
                                    r=[CB_, CCW], w=[dgb])
                            for k in range(k0, k1):
                                S.op("pe", lambda e, k=k: e.matmul(
                                    cbk_[:, 0:TT], dg[:, k, :], gwin[:, j, k:k + TT], start=(k == 0), stop=(k == 30)),
                                    r=[dg, gwin], w=[cbk_], inc=(k == 30))
                            if k1 == 31:
                                S.op("act", lambda e: e.activation(
                                    out=acc[:, 0:TT], in_=cbk_[:, 0:TT], func=AF.Identity, bias=CCB[:, j:j + 1], scale=1.0),
                                    r=[cbk_, CCB], w=[acc])
                        pend4.append(cjob)
                        pend4.append(lambda: WLo.pump(1))

                def pump4(n=1):
                    for _ in range(n):
                        if pend4:
                            pend4.pop(0)()
                pump4(1)
                def ln_part(TT=TT):
                    pump4(len(pend4))
                    mb, qbk = B[0], B[1]
                    for j in range(8):
                        acc = cacc[j]
                        v_, q_ = vb[j % 2], qb[j % 2]
                        S.op("act", lambda e, acc=acc, v_=v_: e.activation(out=v_[:, 0:TT], in_=acc[:, 0:TT], func=AF.Copy),
                             r=[acc], w=[v_])
                        S.op("act", lambda e, acc=acc, q_=q_: e.activation(out=q_[:, 0:TT], in_=acc[:, 0:TT], func=AF.Square),
                             r=[acc], w=[q_])
                        S.op("pe", lambda e, j=j, v_=v_: e.matmul(mb[:, 0:TT], ONESM, v_[:, 0:TT], start=(j == 0), stop=(j == 7)),
                             r=[CB_, v_], w=[mb], inc=True)
                        S.op("pe", lambda e, j=j, q_=q_: e.matmul(qbk[:, 0:TT], ONESM, q_[:, 0:TT], start=(j == 0), stop=(j == 7)),
                             r=[CB_, q_], w=[qbk], inc=True)
                    S.op("act", lambda e: e.activation(out=mean_sb[:, 0:TT], in_=mb[:, 0:TT], func=AF.Copy), r=[mb], w=[mean_sb])
                    S.op("dve", lambda e: e.tensor_tensor(out=rstd_sb[:, 0:TT], in0=mean_sb[:, 0:TT], in1=mean_sb[:, 0:TT],
                                                          op=ALU.mult), r=[mean_sb], w=[rstd_sb])
                    S.op("dve", lambda e: e.tensor_tensor(out=rstd_sb[:, 0:TT], in0=qbk[:, 0:TT], in1=rstd_sb[:, 0:TT],
                                                          op=ALU.subtract), r=[qbk, rstd_sb], w=[rstd_sb])
                    S.op("act", lambda e: e.activation(out=rstd_sb[:, 0:TT], in_=rstd_sb[:, 0:TT], func=AF.Sqrt, bias=EPSC[:, 0:1],
                                                       scale=1.0), r=[rstd_sb, EPSC], w=[rstd_sb])
                    S.op("dve", lambda e: e.reciprocal(out=rstd_sb[:, 0:TT], in_=rstd_sb[:, 0:TT]), r=[rstd_sb], w=[rstd_sb])
                    u_ = uT[0]
                    for j in range(8):
                        eng = "dve"
                        acc = cacc[j]
                        S.op(eng, lambda e, acc=acc: e.tensor_tensor(out=acc[:, 0:TT], in0=acc[:, 0:TT], in1=mean_sb[:, 0:TT],
                                                                     op=ALU.subtract), r=[acc, mean_sb], w=[acc])
                        S.op(eng, lambda e, acc=acc: e.tensor_tensor(out=acc[:, 0:TT], in0=acc[:, 0:TT], in1=rstd_sb[:, 0:TT],
                                                                     op=ALU.mult), r=[acc, rstd_sb], w=[acc])
                        S.op("act", lambda e, j=j, acc=acc: e.activation(out=u_[:, j, 0:TT], in_=acc[:, 0:TT], func=AF.Silu,
                                                                         bias=LNB[:, j:j + 1], scale=LNG[:, j:j + 1]),
                             r=[acc, LNB, LNG], w=[u_])

                for blk in range(nb - 1, -1, -1):
                    if blk == 0 and nb > 1:
                        ln_part()
                    ch = (tok0[s] + t0) // 128 + blk
                    ct = cts[cn % 2]
                    yp = yps[cn % 2]
                    zt = zts[0]
                    cn += 1
                    g0 = ch * 128
                    S.dma("sp", ct[:], CTS[ch], w=[ct])
                    S.dma("sp", yp[:], YP[g0:g0 + 128, :], w=[yp])
                    S.dma("sp", zt[:], ZS[g0:g0 + 128, :], w=[zt])

                    def yfin(g, Yg, t1, yp=yp):
                        gs = slice(g * 256, (g + 1) * 256)
                        S.op("dve", lambda e: e.tensor_tensor(out=yp[:, gs], in0=Yg[:, 0:256], in1=t1[:], op=ALU.add),
                             r=[Yg, t1], w=[yp])

                    ssd_chunk(1, ch, ct[:, 0:1024], ct[:, 1024:1536],
                              lambda g, ct=ct: ct[:, 1536 + g * 128:1536 + (g + 1) * 128],
                              lambda g, ct=ct: ct[:, 2048 + g * 128:2048 + (g + 1) * 128],
                              [ct], H, Hb, Wk, yfin, hook=pump4, ypacc=yp)
                    S.op("dve", lambda e, yp=yp, zt=zt: e.tensor_tensor(out=yp[:], in0=yp[:], in1=zt[:], op=ALU.mult),
                         r=[yp, zt], w=[yp])
                    S.op("act", lambda e, yp=yp: e.activation(out=junk[:], in_=yp[:], func=AF.Square, accum_out=ss[:, 0:1]),
                         r=[yp], w=[junk, ss])
                    S.op("act", lambda e: e.activation(out=ss[:, 1:2], in_=ss[:, 0:1], func=AF.Sqrt, bias=EPSC[:, 0:1], scale=1.0 / D), r=[ss, EPSC], w=[ss])
                    S.op("dve", lambda e: e.reciprocal(out=ss[:, 2:3], in_=ss[:, 1:2]), r=[ss], w=[ss])
                    ynb = ynbs[cn % 2]
                    S.op("dve", lambda e, yp=yp, ynb=ynb: e.scalar_tensor_tensor(out=ynb[:], in0=yp[:], scalar=ss[:, 2:3], in1=NGB[:],
                                                                                 op0=ALU.mult, op1=ALU.mult), r=[yp, ss, NGB], w=[ynb])
                    def post_b(blk=blk, ynb=ynbs[cn % 2]):
                        for k in range(8):
                            S.op("pe", lambda e, k=k: e.transpose(out=TB[:, k * 128:(k + 1) * 128],
                                                                  in_=ynb[:, k * 128:(k + 1) * 128], identity=IDB),
                                 r=[ynb, CB_], w=[TB], inc=(k == 7))
                        S.op("act", lambda e: e.activation(
                            out=yT[:, :, blk * 128:(blk + 1) * 128], in_=TB[:, :].rearrange("p (k t) -> p k t", k=8),
                            func=AF.Copy), r=[TB], w=[yT])
                    if blk > 0:
                        pend4.insert(0, post_b)
                    else:
                        post_b()
                u_ = uT[0]
                if nb == 1:
                    ln_part()
                WLo.all()
                for blk in range(nb):
                    g0 = tok0[s] + t0 + blk * 128
                    xt = xts[0]
                    xo = xos[0]
                    xn += 1
                    S.dma("sp", xt[:], x_in[g0:g0 + 128, :], w=[xt])
                    for half in range(2):
                        bk = B[3 + half]
                        hs = slice(half * 512, (half + 1) * 512)
                        for k in range(16):
                            lhs = yT[:, k, blk * 128:(blk + 1) * 128] if k < 8 else u_[:, k - 8, blk * 128:(blk + 1) * 128]
                            S.op("pe", lambda e, k=k, lhs=lhs, hs=hs, bk=bk: e.matmul(
                                bk[:, :], lhs, WO[:, k, hs], start=(k == 0), stop=(k == 15)),
                                r=[yT, u_, WLo.rb(0)], w=[bk], inc=(k == 15))
                        tt = tts[half]
                        S.op("dve", lambda e, hs=hs, bk=bk, tt=tt: e.tensor_tensor(
                            out=tt[:], in0=bk[:, :], in1=GT1[:, hs], op=ALU.mult), r=[bk, GT1], w=[tt])
                        S.op("pool", lambda e, hs=hs, tt=tt, xt=xt, xo=xo: e.tensor_tensor(
                            out=xo[:, hs], in0=tt[:], in1=xt[:, hs], op=ALU.add), r=[tt, xt], w=[xo])
                    S.dma("sp", XA[g0:g0 + 128, :], xo[:], r=[xo])
            if kind == "p":
                final_state(pidx, 1, H, stg)
                pidx += 1
        S.barrier()

    if upto == "k4":
        return finish()
    ffn(0, XA, XB, False)
    if upto == "k6":
        return finish()

    with contextlib.ExitStack() as ph:
        FWB = sb([128, 8, D], BF16, ph)
        WLf = WLoad(FWB, fnet_w, 8, D)
        FBB = sb([128, D], F32, ph)
        S.dma("sp", FBB[:], bc(fnet_b), w=[FBB])
        CT7 = CondTiles(ph, [(1, 2)])
        GT1 = CT7.t[0]
        Lmax = max(Ls)
        KCm = Lmax // 128
        HM = [sb([128, D], BF16, ph) for _ in range(KCm)]
        xc = [0]
        ln = 0
        jn = 0
        xn = 0
        for s, (kind, L) in enumerate(seqs):
            CT7.set(cond_of[s])
            KC = L // 128
            with contextlib.ExitStack() as pha:
                CT7a = CondTiles(pha, [(1, 1), (1, 0)])
                CT7a.set(cond_of[s])
                xtsa = [sb([128, D], F32, pha) for _ in range(2)]
                nm = (sb([128, D], BF16, pha), sb([128, 4], F32, pha), sb([128, D], F32, pha))
                for kc in range(KC):
                    make_hT(pha, XB, s, kc * 128, 128, CT7a.t[0], CT7a.t[1], xtsa, nm, HM[kc], None, xc)
                S.barrier()
            WLf.all()
            phb = contextlib.ExitStack()
            CLt = [sb([128, KC, 256], BF16, phb) for _ in range(2)]
            SLt = [sb([128, KC, 256], BF16, phb) for _ in range(2)]
            pcs = [sb([128, 512], BF16, phb) for _ in range(2)]
            uTt = [sb([128, 8, 256], BF16, phb) for _ in range(2)]
            tts = [sb([128, 512], F32, phb) for _ in range(2)]
            xos = [sb([128, D], F32, phb) for _ in range(1)]
            xts = [sb([128, D], F32, phb) for _ in range(1)]
            scale = 1.0 / float(np.sqrt(L * 128.0))
            NLT = L // 256

            def ld_dft(lt):
                S.dma("sp", CLt[lt % 2][:, :, :], dftL[L][0][lt], w=[CLt[lt % 2]])
                S.dma("sp", SLt[lt % 2][:, :, :], dftL[L][1][lt], w=[SLt[lt % 2]])
            ld_dft(0)
            for lt in range(NLT):
                cl, sl_ = CLt[lt % 2], SLt[lt % 2]
                uT_ = uTt[ln % 2]
                ln += 1
                if lt + 1 < NLT:
                    ld_dft(lt + 1)
                for j in range(8):
                    bk = B[jn % 2]
                    pc = pcs[jn % 2]
                    ub = B[2 + jn % 2]
                    jn += 1
                    for (o, mat) in ((0, cl), (256, sl_)):
                        for kc in range(KC):
                            S.op("pe", lambda e, o=o, mat=mat, kc=kc, j=j, bk=bk: e.matmul(
                                bk[:, o:o + 256], HM[kc][:, j * 128:(j + 1) * 128], mat[:, kc, :],
                                start=(kc == 0), stop=(kc == KC - 1)), r=[HM[kc], mat], w=[bk], inc=(kc == KC - 1))
                    S.op("act", lambda e, bk=bk, pc=pc: e.activation(out=pc[:], in_=bk[:, :], func=AF.Copy), r=[bk], w=[pc])
                    S.op("pe", lambda e, pc=pc, ub=ub: e.matmul(ub[:, 0:256], CCS[:, 0, :], pc[:, 0:256], start=True, stop=False),
                         r=[CCS, pc], w=[ub], inc=False)
                    S.op("pe", lambda e, pc=pc, ub=ub: e.matmul(ub[:, 0:256], CCS[:, 1, :], pc[:, 256:512], start=False, stop=True),
                         r=[CCS, pc], w=[ub], inc=True)
                    S.op("act", lambda e, j=j, ub=ub: e.activation(out=uT_[:, j, :], in_=ub[:, 0:256], func=AF.Copy, scale=scale),
                         r=[ub], w=[uT_])
                for blk in range(2):
                    g0 = tok0[s] + lt * 256 + blk * 128
                    xt = xts[0]
                    xo = xos[0]
                    xn += 1
                    S.dma("sp", xt[:], XB[g0:g0 + 128, :], w=[xt])
                    for half in range(2):
                        bk = B[4 + half]
                        hs = slice(half * 512, (half + 1) * 512)
                        for k in range(8):
                            S.op("pe", lambda e, k=k, blk=blk, hs=hs, bk=bk: e.matmul(
                                bk[:, :], uT_[:, k, blk * 128:(blk + 1) * 128], FWB[:, k, hs],
                                start=(k == 0), stop=(k == 7)), r=[uT_, WLf.rb(0)], w=[bk], inc=(k == 7))
                        tt = tts[half]
                        S.op("dve", lambda e, hs=hs, bk=bk, tt=tt: e.tensor_tensor(
                            out=tt[:], in0=bk[:, :], in1=FBB[:, hs], op=ALU.add), r=[bk, FBB], w=[tt])
                        S.op("dve", lambda e, hs=hs, tt=tt: e.tensor_tensor(
                            out=tt[:], in0=tt[:], in1=GT1[:, hs], op=ALU.mult), r=[tt, GT1], w=[tt])
                        S.op("dve", lambda e, hs=hs, tt=tt, xt=xt, xo=xo: e.tensor_tensor(
                            out=xo[:, hs], in0=tt[:], in1=xt[:, hs], op=ALU.add), r=[tt, xt], w=[xo])
                    S.dma("sp", XA[g0:g0 + 128, :], xo[:], r=[xo])
            S.barrier()
            phb.close()

    if upto == "k7":
        return finish()
    ffn(1, XA, y_out, True)
    return finish()


def _consts(Ls):
    bf = ml_dtypes.bfloat16
    i = np.arange(128)
    sp, q = i[:, None], i[None, :]
    cb = np.zeros((128, 9, 128), np.float32)
    cb[:, 0] = np.eye(128)
    cb[:, 1] = (sp <= q)
    cb[:, 2] = (sp > q)
    cb[:, 3] = (sp >= q)
    cb[:, 4] = (sp < q)
    cb[:, 5] = np.where(q >= sp, 0.0, -30000.0)
    cb[:, 6] = np.where(q <= sp, 0.0, -30000.0)
    cb[:, 7] = 1.0
    cb[:, 8] = 1.0 / 1024.0
    ang = 2.0 * np.pi * (i[:, None] * i[None, :] % 128) / 128.0
    ccs = np.stack([np.cos(ang), -np.sin(ang)], axis=1)
    out = {"consts_bf": cb.astype(bf), "consts_f32": np.eye(128, dtype=np.float32), "dft_c": ccs.astype(bf)}
    for L in Ls:
        l = np.arange(L, dtype=np.int64)
        prod = (l[:, None] * l[None, :]) % L
        a = 2.0 * np.pi * prod.astype(np.float64) / L
        for nm, M in (("dft_cl_%d" % L, np.cos(a)), ("dft_sl_%d" % L, np.sin(a))):
            M4 = M.reshape(L // 128, 128, L // 256, 256).transpose(2, 1, 0, 3)
            out[nm] = np.ascontiguousarray(M4).astype(bf)
    return out


def _shared_inputs(inp):
    f = lambda a: np.ascontiguousarray(np.asarray(a, dtype=np.float32))
    d = {}
    d["w_mod"] = f(inp["w_mod"])
    d["b_mod"] = f(inp["b_mod"]).reshape(1, -1)
    d["g_rows"] = np.concatenate([f(inp["g_mix"]).reshape(-1), f(inp["g_ffn"]).reshape(-1)]).reshape(1, -1)
    d["w_in"] = f(inp["w_in"])[0]
    d["ssd_conv_wT"] = np.ascontiguousarray(f(inp["ssd_conv_w"])[0].reshape(5, 16, 128).transpose(2, 1, 0))
    d["ssd_conv_bT"] = np.ascontiguousarray(f(inp["ssd_conv_b"])[0].reshape(16, 128).T)
    d["dt_bias"] = f(inp["ssd_dt_bias"])[0].reshape(1, 32)
    d["a_log"] = f(inp["ssd_a_log"])[0].reshape(1, 32)
    d["d_row"] = np.repeat(f(inp["ssd_d"])[0], 64).reshape(1, D)
    d["ssd_norm"] = f(inp["ssd_norm"])[0].reshape(1, D)
    d["conf_conv_wT"] = np.ascontiguousarray(f(inp["conf_conv_w"])[0].reshape(31, 8, 128).transpose(2, 1, 0))
    d["conf_conv_bT"] = np.ascontiguousarray(f(inp["conf_conv_b"])[0].reshape(8, 128).T)
    d["ln_gT"] = np.ascontiguousarray(f(inp["conf_ln_g"])[0].reshape(8, 128).T)
    d["ln_bT"] = np.ascontiguousarray(f(inp["conf_ln_b"])[0].reshape(8, 128).T)
    d["w_out"] = f(inp["w_out"])[0]
    d["fnet_w"] = f(inp["fnet_w"])[0]
    d["fnet_b"] = f(inp["fnet_b"])[0].reshape(1, D)
    d["ffn_w_up"] = f(inp["ffn_w_up"])
    fw = f(inp["ffn_conv_w"]).reshape(2, 3, 3, 2, 22, 128)
    fw = np.stack([fw[:, dy, dx] for (dy, dx) in ORDER9], axis=1)
    d["ffn_conv_wT"] = np.ascontiguousarray(fw.transpose(0, 4, 3, 2, 1).reshape(2, 128, 22, 18))
    d["ffn_w_down"] = f(inp["ffn_w_down"])
    d["norm_f"] = f(inp["norm_f"]).reshape(1, D)
    return d


def run(inp, seqs, core_items, trace=False, upto=None, debug=False):
    nc = build(seqs, upto=upto, debug=debug)
    shared = _shared_inputs(inp)
    shared.update(_consts(sorted(set(L for _, L in seqs))))
    xs = np.asarray(inp["x_sample"], np.float32)
    xp = np.asarray(inp["x_prompt"], np.float32)
    cc = np.asarray(inp["c"], np.float32)
    cctx = np.asarray(inp["c_ctx"], np.float32)
    st = np.asarray(inp["state_ssd"], np.float32)
    Ls_ = [L for k, L in seqs if k == "s"][0]
    Lp_ = [L for k, L in seqs if k == "p"][0]
    in_maps = []
    for (si, pis) in core_items:
        m = dict(shared)
        m["x"] = np.ascontiguousarray(np.concatenate([xs[si, :Ls_]] + [xp[p, :Lp_] for p in pis], axis=0))
        cond = np.stack([cc[si], cctx], axis=1)
        m["condT"] = np.ascontiguousarray(cond.reshape(8, 128, 2).transpose(1, 0, 2))
        m["state0"] = np.ascontiguousarray(st[si, 0].reshape(2, D, 128))
        in_maps.append(m)
    res = run_bass_kernel_spmd(nc, in_maps, core_ids=list(range(len(core_items))), trace=trace)
    return res


def kernel(**inp):
    nP = 4
    seqs = [("s", 4096)] + [("p", 256)] * nP
    core_items = [(i, list(range(4 * i, 4 * i + 4))) for i in range(8)]
    res = run(inp, seqs, core_items)
    y_prompt = np.zeros((32, 256, D), np.float32)
    y_sample = np.zeros((8, 4096, D), np.float32)
    new_state = np.zeros((32, 1, 2, 16, 64, 128), np.float32)
    for i, r in enumerate(res.results):
        y = np.asarray(r["y"])
        y_sample[i] = y[:4096]
        y_prompt[4 * i:4 * i + 4] = y[4096:].reshape(4, 256, D)
        new_state[4 * i:4 * i + 4, 0] = np.asarray(r["st"]).reshape(4, 2, 16, 64, 128)
    return (y_prompt, y_sample, new_state)
```
